# Optimizing a Trainium2 kernel written in Bass

```python
import math
import jax, jax.numpy as jnp
from jax import lax
import numpy as np

D_MODEL = 1024
BATCH = 8
SEQ = 4096
DEPTH = 1

GRID_W = 64
CTX_LEN = 256
HEAD_DIM = 64
NA_HEADS = 8
NA_WIDTH = NA_HEADS * HEAD_DIM
NA_WIN_H = 8
NA_WIN_W = 16
HY_WIDTH = D_MODEL // 2
HY_ORDER = 2
HY_SHORT = 3
HY_EMB = 33
HY_BANDS = (HY_EMB - 1) // 2
HY_FFN = 64
HY_FAST_DECAY = 0.3
HY_SLOW_DECAY = 1.5
HY_TARGET = 1e-2
HY_FILTER_SCALE = 0.008
HY_PROJ = (HY_ORDER + 1) * HY_WIDTH
FFN_HIDDEN = -(-(8 * D_MODEL) // (3 * 256)) * 256
N_MOD = 6
ROPE_THETA = 10000.0
RMS_EPS = 1e-6
NEG_INF = -1e30
IN_SPLITS = (NA_WIDTH, 2 * NA_WIDTH, 3 * NA_WIDTH, 3 * NA_WIDTH + HY_WIDTH,
             3 * NA_WIDTH + 2 * HY_WIDTH, 3 * NA_WIDTH + HY_PROJ, 3 * NA_WIDTH + HY_PROJ + D_MODEL)
IN_WIDTH = 3 * NA_WIDTH + HY_PROJ + 2 * D_MODEL

kernel_name = 'hybrid_na_hyena_dit_block'


def rmsnorm(x, g):
    xf = x.astype(jnp.float32)
    y = xf * lax.rsqrt(jnp.mean(xf * xf, axis=-1, keepdims=True) + RMS_EPS)
    return (y * g.astype(jnp.float32)).astype(x.dtype)


def modulate(h, shift, scale):
    return h * (1 + scale) + shift


def to_heads(a):
    B, L, _ = a.shape
    return a.reshape(B, L, NA_HEADS, HEAD_DIM)


def axial_rope(x, rows, cols):
    nf = HEAD_DIM // 4
    inv = ROPE_THETA ** (-jnp.arange(nf, dtype=jnp.float32) / nf)

    def rot(xp, pos):
        ang = pos.astype(jnp.float32)[:, None] * inv
        cos = jnp.cos(ang)[None, :, None, :]
        sin = jnp.sin(ang)[None, :, None, :]
        x1 = xp[..., :nf].astype(jnp.float32)
        x2 = xp[..., nf:].astype(jnp.float32)
        return jnp.concatenate([x1 * cos - x2 * sin, x2 * cos + x1 * sin], axis=-1)

    half = HEAD_DIM // 2
    return jnp.concatenate([rot(x[..., :half], rows), rot(x[..., half:], cols)], axis=-1).astype(x.dtype)


def neighbourhood_attention(q, k, v, k_ctx, v_ctx, rpb):
    B, L, H, Dh = q.shape
    R = L // GRID_W
    wh = min(NA_WIN_H, R)
    scale = Dh ** -0.5
    t = jnp.arange(L)
    rows, cols = t // GRID_W, t % GRID_W
    q_rot = axial_rope(q, rows, cols)
    k_rot = axial_rope(k, rows, cols)

    def grid(a):
        return a.reshape(B, R, GRID_W, H, Dh)

    r = jnp.arange(R)
    row_idx = jnp.clip(r - wh // 2, 0, R - wh)[:, None] + jnp.arange(wh)[None, :]
    k_band = grid(k_rot)[:, row_idx]
    v_band = grid(v)[:, row_idx]
    cq = jnp.arange(GRID_W)
    start_c = jnp.clip(cq - NA_WIN_W // 2, 0, GRID_W - NA_WIN_W)
    col_ok = (cq[None, :] >= start_c[:, None]) & (cq[None, :] < start_c[:, None] + NA_WIN_W)
    roff = row_idx - r[:, None] + (NA_WIN_H - 1)
    coff = jnp.clip(cq[None, :] - cq[:, None], -(NA_WIN_W - 1), NA_WIN_W - 1) + (NA_WIN_W - 1)
    bias = rpb.astype(jnp.float32)[:, roff[:, None, :, None], coff[None, :, None, :]]
    s_nb = jnp.einsum('brqhd,brikhd->bhrqik', grid(q_rot), k_band,
                      preferred_element_type=jnp.float32) * scale + bias[None]
    s_nb = jnp.where(col_ok[:, None, :], s_nb, NEG_INF)
    s_ctx = jnp.einsum('brqhd,bchd->bhrqc', grid(q), k_ctx, preferred_element_type=jnp.float32) * scale
    s = jnp.concatenate([s_nb.reshape(B, H, R, GRID_W, wh * GRID_W), s_ctx], axis=-1)
    p = jax.nn.softmax(s, axis=-1).astype(v.dtype)
    p_nb = p[..., :wh * GRID_W].reshape(B, H, R, GRID_W, wh, GRID_W)
    p_ctx = p[..., wh * GRID_W:]
    o = (jnp.einsum('bhrqik,brikhd->brqhd', p_nb, v_band)
         + jnp.einsum('bhrqc,bchd->brqhd', p_ctx, v_ctx))
    return o.reshape(B, L, H * Dh)


def context_attention(q, k, v):
    B, Lc, H, Dh = q.shape
    s = jnp.einsum('bqhd,bkhd->bhqk', q, k, preferred_element_type=jnp.float32) * (Dh ** -0.5)
    p = jax.nn.softmax(s, axis=-1).astype(v.dtype)
    return jnp.einsum('bhqk,bkhd->bqhd', p, v).reshape(B, Lc, H * Dh)


def short_conv(u, w, b):
    up = jnp.pad(u, ((0, 0), (1, 1), (0, 0)))
    return up[:, :-2] * w[0] + up[:, 1:-1] * w[1] + up[:, 2:] * w[2] + b


def hyena_filters(L, w1, b1, w2, b2, freq, w3):
    f32 = jnp.float32
    t = jnp.linspace(0.0, 1.0, L, dtype=f32)[:, None]
    w = 2.0 * math.pi * jnp.arange(L, dtype=f32)[:, None] / L
    bands = jnp.linspace(1e-4, HY_BANDS - 1, HY_BANDS, dtype=f32)
    z = jnp.concatenate([t, jnp.cos(bands * w), jnp.sin(-bands * w)], axis=-1)
    fr = freq.astype(f32)
    hid = jnp.sin(fr * (z @ w1.astype(f32) + b1.astype(f32)))
    hid = jnp.sin(fr * (hid @ w2.astype(f32) + b2.astype(f32)))
    h = (hid @ w3.astype(f32)).reshape(L, 2, HY_ORDER, HY_WIDTH)
    min_decay = math.log(HY_TARGET) / HY_SLOW_DECAY
    max_decay = math.log(HY_TARGET) / HY_FAST_DECAY
    deltas = jnp.abs(jnp.linspace(min_decay, max_decay, HY_WIDTH, dtype=f32))
    decay = jnp.exp(-t * deltas)
    return h * decay[:, None, None, :]


def long_conv_bidir(u, h_fwd, h_bwd, bias):
    B, L, C = u.shape
    k2 = jnp.concatenate([h_fwd, jnp.zeros((1, C), jnp.float32), h_bwd[:0:-1]], axis=0)
    U = jnp.fft.rfft(u.astype(jnp.float32), n=2 * L, axis=1)
    K = jnp.fft.rfft(k2, n=2 * L, axis=0)
    y = jnp.fft.irfft(U * K[None], n=2 * L, axis=1)[:, :L]
    return (y + u.astype(jnp.float32) * bias.astype(jnp.float32)).astype(u.dtype)


def hyena_mixer(hv, hx1, hx2, conv_w, conv_b, w1, b1, w2, b2, freq, w3, bias):
    u = short_conv(jnp.concatenate([hv, hx1, hx2], axis=-1), conv_w, conv_b)
    parts = jnp.split(u, HY_ORDER + 1, axis=-1)
    z, gates = parts[0], parts[1:]
    filt = hyena_filters(z.shape[1], w1, b1, w2, b2, freq, w3)
    for n in range(HY_ORDER):
        z = gates[n] * long_conv_bidir(z, filt[:, 0, n], filt[:, 1, n], bias[n])
    return z


def gated_merge(g_na, g_hy, y_na, y_hy, w_na_o, w_hy_o, w_out):
    m = jax.nn.sigmoid(g_na) * (y_na @ w_na_o) + jax.nn.sigmoid(g_hy) * (y_hy @ w_hy_o)
    return m @ w_out


def swiglu(h, w1, w3, w2):
    return (jax.nn.silu(h @ w1) * (h @ w3)) @ w2


def setup_inputs(seed: int = 0) -> dict:
    key = jax.random.key(seed)
    ks = jax.random.split(key, 32)
    f32 = jnp.float32

    def nrm(k, shape, scale):
        return jax.random.normal(k, shape, f32) * scale

    return {
        'x': nrm(ks[0], (BATCH, SEQ, D_MODEL), 1.0),
        'c': nrm(ks[1], (BATCH, D_MODEL), 1.0),
        'ctx': nrm(ks[2], (BATCH, CTX_LEN, D_MODEL), 1.0),
        'c_ctx': nrm(ks[3], (D_MODEL,), 1.0),
        'w_ada': nrm(ks[4], (DEPTH, D_MODEL, N_MOD * D_MODEL), D_MODEL ** -0.5),
        'b_ada': nrm(ks[5], (DEPTH, N_MOD * D_MODEL), 0.02),
        'norm1_g': 1.0 + nrm(ks[6], (DEPTH, D_MODEL), 0.02),
        'norm2_g': 1.0 + nrm(ks[7], (DEPTH, D_MODEL), 0.02),
        'w_in': nrm(ks[8], (DEPTH, D_MODEL, IN_WIDTH), D_MODEL ** -0.5),
        'na_rpb': nrm(ks[9], (DEPTH, NA_HEADS, 2 * NA_WIN_H - 1, 2 * NA_WIN_W - 1), 0.1),
        'hy_conv_w': nrm(ks[10], (DEPTH, HY_SHORT, HY_PROJ), HY_SHORT ** -0.5),
        'hy_conv_b': nrm(ks[11], (DEPTH, HY_PROJ), 0.02),
        'hy_ffn_w1': nrm(ks[12], (DEPTH, HY_EMB, HY_FFN), HY_EMB ** -0.5),
        'hy_ffn_b1': nrm(ks[13], (DEPTH, HY_FFN), 0.1),
        'hy_ffn_w2': nrm(ks[14], (DEPTH, HY_FFN, HY_FFN), HY_FFN ** -0.5),
        'hy_ffn_b2': nrm(ks[15], (DEPTH, HY_FFN), 0.1),
        'hy_sin_freq': 1.0 + nrm(ks[16], (DEPTH, HY_FFN), 0.05),
        'hy_ffn_w3': nrm(ks[17], (DEPTH, HY_FFN, 2 * HY_ORDER * HY_WIDTH), HY_FILTER_SCALE),
        'hy_bias': nrm(ks[18], (DEPTH, HY_ORDER, HY_WIDTH), 0.2),
        'w_na_o': nrm(ks[19], (DEPTH, NA_WIDTH, D_MODEL), NA_WIDTH ** -0.5),
        'w_hy_o': nrm(ks[20], (DEPTH, HY_WIDTH, D_MODEL), HY_WIDTH ** -0.5),
        'w_out': nrm(ks[21], (DEPTH, D_MODEL, D_MODEL), D_MODEL ** -0.5),
        'ffn_w1': nrm(ks[22], (DEPTH, D_MODEL, FFN_HIDDEN), D_MODEL ** -0.5),
        'ffn_w3': nrm(ks[23], (DEPTH, D_MODEL, FFN_HIDDEN), D_MODEL ** -0.5),
        'ffn_w2': nrm(ks[24], (DEPTH, FFN_HIDDEN, D_MODEL), FFN_HIDDEN ** -0.5),
        'final_g': 1.0 + nrm(ks[25], (D_MODEL,), 0.02),
    }


def reference(x, c, ctx, c_ctx, w_ada, b_ada, norm1_g, norm2_g, w_in, na_rpb, hy_conv_w, hy_conv_b,
              hy_ffn_w1, hy_ffn_b1, hy_ffn_w2, hy_ffn_b2, hy_sin_freq, hy_ffn_w3, hy_bias,
              w_na_o, w_hy_o, w_out, ffn_w1, ffn_w3, ffn_w2, final_g):
    xc = ctx
    for i in range(DEPTH):
        last = i == DEPTH - 1
        mod = jax.nn.silu(c) @ w_ada[i] + b_ada[i]
        mod_c = jax.nn.silu(c_ctx) @ w_ada[i] + b_ada[i]
        sh1, sc1, g1, sh2, sc2, g2 = jnp.split(mod[:, None, :], N_MOD, axis=-1)
        csh1, csc1, cg1, csh2, csc2, cg2 = jnp.split(mod_c, N_MOD, axis=-1)
        hy_params = (hy_conv_w[i], hy_conv_b[i], hy_ffn_w1[i], hy_ffn_b1[i], hy_ffn_w2[i], hy_ffn_b2[i],
                     hy_sin_freq[i], hy_ffn_w3[i], hy_bias[i])

        hc = modulate(rmsnorm(xc, norm1_g[i]), csh1, csc1)
        if last:
            k_c, v_c = jnp.split(hc @ w_in[i][:, NA_WIDTH:3 * NA_WIDTH], 2, axis=-1)
        else:
            q_c, k_c, v_c, hv_c, hx1_c, hx2_c, gna_c, ghy_c = jnp.split(hc @ w_in[i], IN_SPLITS, axis=-1)
            y_na_c = context_attention(to_heads(q_c), to_heads(k_c), to_heads(v_c))
            y_hy_c = hyena_mixer(hv_c, hx1_c, hx2_c, *hy_params)
            xc_next = xc + cg1 * gated_merge(gna_c, ghy_c, y_na_c, y_hy_c, w_na_o[i], w_hy_o[i], w_out[i])
            hc2 = modulate(rmsnorm(xc_next, norm2_g[i]), csh2, csc2)
            xc_next = xc_next + cg2 * swiglu(hc2, ffn_w1[i], ffn_w3[i], ffn_w2[i])

        h = modulate(rmsnorm(x, norm1_g[i]), sh1, sc1)
        q, k, v, hv, hx1, hx2, gna, ghy = jnp.split(h @ w_in[i], IN_SPLITS, axis=-1)
        y_na = neighbourhood_attention(to_heads(q), to_heads(k), to_heads(v),
                                       to_heads(k_c), to_heads(v_c), na_rpb[i])
        y_hy = hyena_mixer(hv, hx1, hx2, *hy_params)
        x = x + g1 * gated_merge(gna, ghy, y_na, y_hy, w_na_o[i], w_hy_o[i], w_out[i])
        h2 = modulate(rmsnorm(x, norm2_g[i]), sh2, sc2)
        x = x + g2 * swiglu(h2, ffn_w1[i], ffn_w3[i], ffn_w2[i])
        if not last:
            xc = xc_next
    return rmsnorm(x, final_g)
```

```python
import math
import numpy as np
import ml_dtypes
import concourse.bass as bass
import concourse.mybir as mybir
from concourse.bass_utils import run_bass_kernel_spmd

F32 = mybir.dt.float32
BF16 = mybir.dt.bfloat16
U8 = mybir.dt.uint8
AF = mybir.ActivationFunctionType
ALU = mybir.AluOpType
NPBF = ml_dtypes.bfloat16

D = 1024
L = 4096
NT = 32
CTX = 256
EPS = 1e-6
NDMA = 8


class Tok:
    __slots__ = ("sem", "val")

    def __init__(self, sem, val):
        self.sem = sem
        self.val = val


class Buf:
    def __init__(self, name="", excl=False):
        self.name = name
        self.w = None
        self.readers = {}
        self.excl = excl
        self.w_is_read = False
        self.w_dma = False
        self.w_extra = []


class Eng:
    def __init__(self, name, sem, dsems=None):
        self.name = name
        self.sem = sem
        self.n = 0
        self.prog = []
        self.waited = {}
        self.dsems = dsems or []
        self.ndma = 0
        self.dtoks = []

    def wait(self, tok):
        if tok is None:
            return
        k = id(tok.sem)
        if self.waited.get(k, 0) < tok.val:
            self.waited[k] = tok.val
            self.prog.append(("w", tok.sem, tok.val))

    def _deps(self, R, W):
        for b in list(R) + list(W):
            for t in b.w_extra:
                self.wait(t)
        for b in R:
            if b.w is not None:
                if self.name == "pe" and b.w.sem is self.sem:
                    continue
                if b.excl and b.w_is_read and b.w.sem is self.sem:
                    continue
                self.wait(b.w)
        for b in W:
            if b.w is not None:
                if not (self.name == "pe" and b.w.sem is self.sem):
                    self.wait(b.w)
            for en, t in b.readers.items():
                if en != self.name or self.name != "pe":
                    self.wait(t)

    def _mark(self, tok, R, W):
        for b in R:
            if b.excl:
                b.w = tok
                b.w_is_read = True
                b.readers = {}
            else:
                b.readers[self.name] = tok
        for b in W:
            b.w = tok
            b.w_is_read = False
            b.w_dma = False
            b.w_extra = []
            b.readers = {}

    def op(self, fn, R=(), W=()):
        self._deps(R, W)
        self.n += 1
        tok = Tok(self.sem, self.n)
        self.prog.append(("i", fn, self.sem, 1))
        self._mark(tok, R, W)
        return tok

    def dma(self, fn, R=(), W=()):
        i = self.ndma
        self.ndma += 1
        s = self.dsems[i % len(self.dsems)]
        if i >= len(self.dsems):
            self.wait(self.dtoks[i - len(self.dsems)])
        self._deps(R, W)
        tok = Tok(s, 16 * (i // len(self.dsems) + 1))
        self.dtoks.append(tok)
        self.prog.append(("i", fn, s, 16))
        for b in R:
            b.readers[self.name + "_dma%d" % i] = tok
        for b in W:
            if b.w_dma and b.w is not None:
                b.w_extra.append(b.w)
            else:
                b.w_extra = []
            b.w = tok
            b.w_dma = True
            b.w_is_read = False
            b.readers = {}
        return tok

    def replay(self, h):
        for it in self.prog:
            if it[0] == "w":
                h.wait_ge(it[1], it[2])
            else:
                it[1](h).then_inc(it[2], it[3])


_CONST = None


def _bf(a):
    return np.ascontiguousarray(np.asarray(a, dtype=np.float32).astype(NPBF))


def host_consts():
    global _CONST
    if _CONST is not None:
        return _CONST
    c = {}
    c["ident"] = _bf(np.eye(128))
    c["identf"] = np.eye(128, dtype=np.float32)
    c["onesf"] = np.ones((128, 128), np.float32)
    nf = 16
    inv = 10000.0 ** (-np.arange(nf, dtype=np.float64) / nf)
    t = np.arange(L)
    rows, cols = t // 64, t % 64
    cos = np.zeros((64, L))
    sin = np.zeros((64, L))
    for d in range(64):
        pos = rows if d < 32 else cols
        dd = d % 32
        f = dd % 16
        ang = pos * inv[f]
        cos[d] = np.cos(ang)
        sin[d] = -np.sin(ang) if dd < 16 else np.sin(ang)
    c["ropecos"] = _bf(np.concatenate([cos, cos], 0))
    c["ropesin"] = _bf(np.concatenate([sin, sin], 0))
    N = 8192
    ns = np.arange(64)[:, None]
    ks = np.arange(64)[None, :]
    th = 2 * np.pi * ns * ks / 64.0
    d1 = np.stack([np.sin(th), np.cos(th), -np.sin(th)], axis=-1)
    c["hy_D1"] = _bf(d1.reshape(64, 192))
    nf = np.arange(128)[:, None, None]
    ksv = np.arange(64)[None, :, None]
    kf = np.arange(128)[None, None, :]
    ph = 2 * np.pi * nf * (ksv + 64 * kf) / N
    c["hy_Ere"] = _bf(np.cos(ph).reshape(128, 64 * 128))
    c["hy_Eim"] = _bf((-np.sin(ph)).reshape(128, 64 * 128))
    kfv = np.arange(128)[:, None]
    na = np.arange(128)[None, :]
    a2 = 2 * np.pi * kfv * na / 128.0
    Cf, Sf = np.cos(a2), np.sin(a2)
    c["hy_G"] = _bf(np.concatenate([Cf, Sf, -Sf, Cf], axis=1))
    ksc = np.arange(64)[:, None, None]
    nav = np.arange(128)[None, :, None]
    nbv = np.arange(32)[None, None, :]
    a3 = 2 * np.pi * ksc * (nav + 128 * nbv) / N
    wk = np.where((ksc == 0) | (ksc == 32), 1.0, np.where(ksc < 32, 2.0, 0.0))
    Tm = np.stack([wk * np.cos(a3) / N, -wk * np.sin(a3) / N], axis=1)
    Tm = Tm.reshape(64, 2 * 128 * 32)
    c["hy_T"] = _bf(np.concatenate([Tm, Tm], axis=0))
    tpos = np.concatenate([np.arange(4096), 4096 - np.arange(4096)])
    tpos[4096] = 0
    tlin = np.linspace(0.0, 1.0, L, dtype=np.float32).astype(np.float64)
    tn = tlin[np.minimum(tpos, L - 1)]
    w = (2.0 * math.pi * np.arange(L, dtype=np.float32) / L).astype(np.float64)[np.minimum(tpos, L - 1)]
    bands = np.linspace(1e-4, 15, 16, dtype=np.float32).astype(np.float64)
    z = np.concatenate([tn[:, None], np.cos(bands[None, :] * w[:, None]), np.sin(-bands[None, :] * w[:, None])], axis=1)
    c["hy_zT"] = np.ascontiguousarray(z.T.astype(np.float32))
    deltas = np.abs(np.linspace(math.log(1e-2) / 1.5, math.log(1e-2) / 0.3, 512, dtype=np.float32)).astype(np.float64)
    dec = np.exp(-tn[:, None] * deltas[None, :])
    dec = dec.reshape(64, 128, 8, 64).transpose(2, 0, 1, 3)
    c["hy_dec"] = _bf(dec.reshape(8, 64, 128 * 64))
    _CONST = c
    return c


def rope_perm():
    p = np.zeros(64, np.int64)
    for d in range(64):
        base = (d // 32) * 32
        dd = d % 32
        p[d] = base + (dd + 16) % 32
    return p


def qk_perm_cols():
    p = rope_perm()
    cols = []
    for blk in range(16):
        cols.extend((blk * 64 + p).tolist())
    return np.asarray(cols, np.int64)


def na_bias_table(rpb):
    cq = np.arange(64)
    ck = np.arange(64)
    start = np.clip(cq - 8, 0, 48)
    ok = (ck[None, :] >= start[:, None]) & (ck[None, :] < start[:, None] + 16)
    coff = np.clip(ck[None, :] - cq[:, None], -15, 15) + 15
    T = rpb[:, :, coff]
    T = np.where(ok[None, None], T, np.float32(-1e30))
    T = np.ascontiguousarray(T.transpose(2, 0, 1, 3)).astype(np.float32)
    T = T.reshape(64, 8 * 15 * 64)
    return np.ascontiguousarray(np.concatenate([T, T], axis=0))


def build_nc(dbg=False, mode="full"):
    nc = bass.Bass("TRN2", target_bir_lowering=False)

    def din(name, shape, dt=F32):
        return nc.dram_tensor(name, list(shape), dt, kind="ExternalInput").ap()

    x_d = din("x", [L, D])
    ctx_d = din("ctx", [CTX, D])
    cc_d = din("cc", [128, 8, 2])
    wada_d = din("w_ada", [D, 6 * D])
    bada_d = din("b_adaT", [128, 48])
    n1g_d = din("n1gT", [128, 8])
    n2g_d = din("n2gT", [128, 8])
    fg_d = din("fg_row", [128, D])
    win_d = din("w_in", [D, 5120])
    wnao_d = din("w_na_o", [512, D])
    whyo_d = din("w_hy_o", [512, D])
    wout_d = din("w_out", [D, D])
    w1_d = din("ffn_w1", [D, 2816])
    w3_d = din("ffn_w3", [D, 2816])
    w2_d = din("ffn_w2", [2816, D])
    ident_d = din("ident", [128, 128], BF16)
    identf_d = din("identf", [128, 128])
    onesf_d = din("onesf", [128, 128])
    wqkp_d = din("w_qk_perm", [D, D])
    rcos_d = din("ropecos", [128, L], BF16)
    rsin_d = din("ropesin", [128, L], BF16)
    bias_d = din("na_biasT", [128, 8 * 15 * 64])
    cw_d = din("hy_cwT", [128, 12, 3])
    cb_d = din("hy_cbT", [128, 12])
    hyD1_d = din("hy_D1", [64, 192], BF16)
    hyEre_d = din("hy_Ere", [128, 64 * 128], BF16)
    hyEim_d = din("hy_Eim", [128, 64 * 128], BF16)
    hyG_d = din("hy_G", [128, 512], BF16)
    hyT_d = din("hy_T", [128, 2 * 128 * 32], BF16)
    hyz_d = din("hy_zT", [33, 8192])
    hydec_d = din("hy_dec", [8, 64, 128 * 64], BF16)
    hyw1_d = din("hy_w1", [33, 64])
    hyw2_d = din("hy_w2", [64, 64])
    hyw3_d = din("hy_w3", [64, 2048])
    hyv_d = din("hy_vecs", [64, 3])
    hyb_d = din("hy_biasT", [64, 16])
    hybr_d = din("hy_bias_row", [1, 1024])
    uT_d = nc.dram_tensor("uTs", [1536, L], BF16, kind="Internal").ap()
    yna_d = nc.dram_tensor("ynas", [512, L], BF16, kind="Internal").ap()
    yhy_d = nc.dram_tensor("yhys", [512, L], BF16, kind="Internal").ap()
    out_d = nc.dram_tensor("out", [L, D], F32, kind="ExternalOutput").ap()
    x1_d = nc.dram_tensor("x1s", [L, D], F32, kind="Internal").ap()
    dbg_d = None
    if dbg:
        dbg_d = nc.dram_tensor("dbg", [128, 4096], F32, kind="ExternalOutput").ap()
        dbg2_d = nc.dram_tensor("dbg2", [128, 4096], BF16, kind="ExternalOutput").ap()
        dbg3_d = nc.dram_tensor("dbg3", [64, 4096], BF16, kind="ExternalOutput").ap()
        dbg4_d = nc.dram_tensor("dbg4", [64, 8192], BF16, kind="ExternalOutput").ap()
        dbg5_d = nc.dram_tensor("dbg5", [128, 8192], BF16, kind="ExternalOutput").ap()

    SB = (nc.sbuf_bytes_remaining - 1024) // 64 * 64
    arena = nc.alloc_sbuf_tensor("arena", [128, SB], U8)
    banks = [nc.alloc_psum_tensor("pb%d" % i, [128, 512], F32) for i in range(8)]
    PB = [Buf("pb%d" % i, excl=True) for i in range(8)]

    class Arena:
        def __init__(self):
            self.off = 0
            self.marks = []
            self.on_pop = None

        def alloc(self, cols, dt, shape=None, parts=128):
            esz = 4 if dt == F32 else 2
            nb = (cols * esz + 63) // 64 * 64
            assert self.off + nb <= SB, ("SBUF overflow", self.off, nb, SB)
            a = arena[0:parts, self.off:self.off + cols * esz].bitcast(dt)
            self.off += nb
            return a

        def push(self):
            self.marks.append(self.off)

        def pop(self):
            self.off = self.marks.pop()
            if self.on_pop is not None:
                self.on_pop()

    ar = Arena()

    sems = {}
    import contextlib
    stack = contextlib.ExitStack()
    with stack:
        def sem(name):
            return stack.enter_context(nc.semaphore(name))

        pe = Eng("pe", sem("s_pe"))
        act = Eng("act", sem("s_act"), [sem("s_actd%d" % i) for i in range(NDMA)])
        dve = Eng("dve", sem("s_dve"))
        pool = Eng("pool", sem("s_pool"), [sem("s_poold%d" % i) for i in range(NDMA)])
        sp = Eng("sp", sem("s_sp"), [sem("s_spd%d" % i) for i in range(NDMA)])

        engs = [pe, act, dve, pool, sp]

        def barrier():
            toks = [Tok(e.sem, e.n) for e in engs if e.n > 0]
            for e in engs:
                toks += e.dtoks[-len(e.dsems):] if e.dsems else []
            for e in engs:
                for t in toks:
                    e.wait(t)

        ar.on_pop = barrier

        def mm(out, lhsT, rhs, start, stop, R, W):
            return pe.op(lambda h: h.matmul(out, lhsT=lhsT, rhs=rhs, start=start, stop=stop), R, W)

        def tr(out, in_, ident, R, W):
            return pe.op(lambda h: h.transpose(out=out, in_=in_, identity=ident), R, W)

        def actf(out, in_, func, R, W, bias=None, scale=None, accum_out=None):
            kw = {}
            if bias is not None:
                kw["bias"] = bias
            if scale is not None:
                kw["scale"] = scale
            if accum_out is not None:
                kw["accum_out"] = accum_out
            return act.op(lambda h: h.activation(out=out, in_=in_, func=func, **kw), R, W)

        def tt(eng, out, in0, in1, op, R, W):
            return eng.op(lambda h: h.tensor_tensor(out=out, in0=in0, in1=in1, op=op), R, W)

        def ts(eng, out, in0, s1, s2, op0, op1, R, W):
            if op1 is None:
                return eng.op(lambda h: h.tensor_scalar(out=out, in0=in0, scalar1=s1, scalar2=None, op0=op0), R, W)
            return eng.op(lambda h: h.tensor_scalar(out=out, in0=in0, scalar1=s1, scalar2=s2, op0=op0, op1=op1), R, W)

        def stt(eng, out, in0, s, in1, op0, op1, R, W):
            return eng.op(lambda h: h.scalar_tensor_tensor(out=out, in0=in0, scalar=s, in1=in1, op0=op0, op1=op1), R, W)

        def cp(eng, out, in_, R, W):
            return eng.op(lambda h: h.tensor_copy(out=out, in_=in_), R, W)

        def dma(eng, out, in_, R, W):
            return eng.dma(lambda h: h.dma_start(out=out, in_=in_), R, W)

        def memset(eng, ap, v, W):
            return eng.op(lambda h: h.memset(ap, v), (), W)

        ident = ar.alloc(128, BF16)
        identf = ar.alloc(128, F32)
        onesf = ar.alloc(128, F32)
        B_const = Buf("const")
        dma(sp, ident, ident_d, (), (B_const,))
        dma(sp, identf, identf_d, (), (B_const,))
        dma(sp, onesf, onesf_d, (), (B_const,))
        cc = ar.alloc(16, F32).rearrange("p (k j) -> p k j", j=2)
        dma(sp, cc, cc_d, (), (B_const,))
        badaT = ar.alloc(48, F32)
        dma(sp, badaT, bada_d, (), (B_const,))
        n1g = ar.alloc(8, F32)
        n2g = ar.alloc(8, F32)
        dma(sp, n1g, n1g_d, (), (B_const,))
        dma(sp, n2g, n2g_d, (), (B_const,))
        fgrow = ar.alloc(D, F32)
        dma(sp, fgrow, fg_d, (), (B_const,))

        B_mod = Buf("mod")
        scc = ar.alloc(16, BF16).rearrange("p (k j) -> p k j", j=2)
        actf(scc, cc, AF.Silu, (B_const,), (B_mod,))
        modT = ar.alloc(96, F32).rearrange("p (c j) -> p c j", j=2)
        ar.push()
        wa = [ar.alloc(8 * 512, BF16).rearrange("p (k n) -> p k n", k=8) for _ in range(2)]
        B_wa = [Buf("wa0"), Buf("wa1")]
        wada_v = wada_d.rearrange("(k p) n -> p k n", p=128)
        for pc in range(12):
            s = pc % 2
            dma(pool, wa[s], wada_v[:, :, pc * 512:(pc + 1) * 512], (), (B_wa[s],))
            pb = pc % 2
            for cj in range(4):
                for kc in range(8):
                    mm(banks[pb][:, cj * 2:cj * 2 + 2], wa[s][:, kc, cj * 128:(cj + 1) * 128], scc[:, kc, :],
                       kc == 0, kc == 7, (B_wa[s], B_mod), (PB[pb],))
            for j in range(2):
                tt(dve, modT[:, pc * 4:(pc + 1) * 4, j], banks[pb][:, j:8:2], badaT[:, pc * 4:(pc + 1) * 4],
                   ALU.add, (PB[pb], B_const), (B_mod,))
        ar.pop()
        A1 = ar.alloc(8, F32)
        A1c = ar.alloc(8, F32)
        A2 = ar.alloc(8, F32)
        stt(dve, A1, modT[:, 8:16, 0], 1.0, n1g, ALU.add, ALU.mult, (B_mod, B_const), (B_mod,))
        stt(dve, A1c, modT[:, 8:16, 1], 1.0, n1g, ALU.add, ALU.mult, (B_mod, B_const), (B_mod,))
        stt(dve, A2, modT[:, 32:40, 0], 1.0, n2g, ALU.add, ALU.mult, (B_mod, B_const), (B_mod,))
        B1 = modT[:, 0:8, 0]
        B1c = modT[:, 0:8, 1]
        B2 = modT[:, 24:32, 0]

        def bcast_rows(vecT, dst):
            ar.push()
            dg = ar.alloc(8 * 128, F32).rearrange("p (k n) -> p k n", k=8)
            B_dg = Buf("dg")
            for kc in range(8):
                ts(dve, dg[:, kc, :], identf, vecT[:, kc:kc + 1], None, ALU.mult, None, (B_mod, B_const), (B_dg,))
            for hh in range(2):
                mm(banks[2 + hh][:, :], onesf, dg[:, hh * 4:(hh + 1) * 4, :].rearrange("p k n -> p (k n)"),
                   True, True, (B_dg, B_const), (PB[2 + hh],))
                actf(dst[:, hh * 512:(hh + 1) * 512], banks[2 + hh][:, :], AF.Copy, (PB[2 + hh],), (B_mod,))
            ar.pop()

        g1row = ar.alloc(D, F32)
        g2row = ar.alloc(D, F32)
        bcast_rows(modT[:, 16:24, 0], g1row)
        bcast_rows(modT[:, 40:48, 0], g2row)

        def rstd_from_ssq(stat, B_stat):
            ts(dve, stat[:, 1:2], stat[:, 0:1], 1.0 / D, EPS, ALU.mult, ALU.add, (B_stat,), (B_stat,))
            actf(stat[:, 3:4], stat[:, 1:2], AF.Sqrt, (B_stat,), (B_stat,))
            dve.op(lambda h: h.reciprocal(out=stat[:, 2:3], in_=stat[:, 3:4]), (B_stat,), (B_stat,))

        def norm_tile(xt, B_xt, xn, B_xn, junk, B_junk, stat, B_stat):
            actf(junk, xt, AF.Square, (B_xt,), (B_junk, B_stat), accum_out=stat[:, 0:1])
            rstd_from_ssq(stat, B_stat)
            ts(dve, xn, xt, stat[:, 2:3], None, ALU.mult, None, (B_xt, B_stat), (B_xn,))

        def transpose_mod(xn, B_xn, pbi, dstT, tok0, B_dst, Asc, Bsc, pbi2=None):
            pbv = banks[pbi][:, :].bitcast(BF16)
            if pbi2 is None:
                for kc in range(8):
                    tr(pbv[:, kc * 128:(kc + 1) * 128], xn[:, kc * 128:(kc + 1) * 128], ident, (B_xn, B_const), (PB[pbi],))
                for kc in range(8):
                    if kc % 2 == 0:
                        actf(dstT[:, kc, tok0:tok0 + 128], pbv[:, kc * 128:(kc + 1) * 128], AF.Identity,
                             (PB[pbi], B_mod), (B_dst,), bias=Bsc[:, kc:kc + 1], scale=Asc[:, kc:kc + 1])
                    else:
                        ts(dve, dstT[:, kc, tok0:tok0 + 128], pbv[:, kc * 128:(kc + 1) * 128], Asc[:, kc:kc + 1],
                           Bsc[:, kc:kc + 1], ALU.mult, ALU.add, (PB[pbi], B_mod), (B_dst,))
                return
            pbv2 = banks[pbi2][:, :].bitcast(BF16)
            for kc in range(8):
                if kc < 5:
                    tr(pbv[:, kc * 128:(kc + 1) * 128], xn[:, kc * 128:(kc + 1) * 128], ident, (B_xn, B_const), (PB[pbi],))
                else:
                    tr(pbv2[:, kc * 128:(kc + 1) * 128], xn[:, kc * 128:(kc + 1) * 128], ident, (B_xn, B_const), (PB[pbi2],))
            for kc in range(5):
                actf(dstT[:, kc, tok0:tok0 + 128], pbv[:, kc * 128:(kc + 1) * 128], AF.Identity,
                     (PB[pbi], B_mod), (B_dst,), bias=Bsc[:, kc:kc + 1], scale=Asc[:, kc:kc + 1])
            for kc in range(5, 8):
                ts(dve, dstT[:, kc, tok0:tok0 + 128], pbv2[:, kc * 128:(kc + 1) * 128], Asc[:, kc:kc + 1],
                   Bsc[:, kc:kc + 1], ALU.mult, ALU.add, (PB[pbi2], B_mod), (B_dst,))

        B_yna = Buf("yna")
        B_ynad = Buf("ynad")
        B_yhyd = [Buf("yhyd%d" % i) for i in range(8)]
        yna_v = yna_d.rearrange("(c p) t -> p c t", p=128)
        yhy_v = yhy_d.rearrange("(c p) t -> p c t", p=128)
        x_v = x_d.rearrange("(n p) d -> n p d", p=128)
        winv = win_d.rearrange("(k p) n -> p k n", p=128)

        ar.push()
        hT = ar.alloc(8 * L, BF16).rearrange("p (k t) -> p k t", k=8)
        ynaP = ar.alloc(L, BF16)
        B_hT = [Buf("hT%d" % i) for i in range(NT)]
        hcT = ar.alloc(8 * CTX, BF16).rearrange("p (k t) -> p k t", k=8)
        B_hcT = Buf("hcT")
        ar.push()
        NB1 = 6
        xt2 = [ar.alloc(D, F32) for _ in range(NB1)]
        B_xt2 = [Buf("xt2_%d" % i) for i in range(NB1)]
        xn2 = [ar.alloc(D, BF16) for _ in range(NB1)]
        B_xn2 = [Buf("xn2_%d" % i) for i in range(NB1)]
        junk = ar.alloc(D, BF16)
        B_junk = Buf("junk1")
        stat = [ar.alloc(4, F32) for _ in range(NB1)]
        B_stat = [Buf("st1_%d" % i) for i in range(NB1)]
        ctx_v = ctx_d.rearrange("(n p) d -> n p d", p=128)
        NTI = NT + 2
        for k in range(NTI + 3):
            k0, k1, k2, k3 = k, k - 1, k - 2, k - 3
            if 0 <= k0 < NTI:
                s = k0 % NB1
                src = x_v[k0] if k0 < NT else ctx_v[k0 - NT]
                dma(sp, xt2[s], src, (), (B_xt2[s],))
                actf(junk, xt2[s], AF.Square, (B_xt2[s],), (B_junk, B_stat[s]), accum_out=stat[s][:, 0:1])
            if 0 <= k1 < NTI:
                s = k1 % NB1
                ts(dve, stat[s][:, 1:2], stat[s][:, 0:1], 1.0 / D, EPS, ALU.mult, ALU.add, (B_stat[s],), (B_stat[s],))
                actf(stat[s][:, 3:4], stat[s][:, 1:2], AF.Sqrt, (B_stat[s],), (B_stat[s],))
            if 0 <= k2 < NTI:
                s = k2 % NB1
                dve.op(lambda h, o_=stat[s][:, 2:3], i_=stat[s][:, 3:4]: h.reciprocal(out=o_, in_=i_), (B_stat[s],), (B_stat[s],))
                ts(dve, xn2[s], xt2[s], stat[s][:, 2:3], None, ALU.mult, None, (B_xt2[s], B_stat[s]), (B_xn2[s],))
            if 0 <= k3 < NTI:
                s = k3 % NB1
                pA, pB = 2 * (k3 % 4 // 1 % 4 % 4) % 8, 0
                pA = 2 * (k3 % 4)
                pB = pA + 1
                if k3 < NT:
                    transpose_mod(xn2[s], B_xn2[s], pA, hT, k3 * 128, B_hT[k3], A1, B1, pbi2=pB)
                else:
                    transpose_mod(xn2[s], B_xn2[s], pA, hcT, (k3 - NT) * 128, B_hcT, A1c, B1c, pbi2=pB)
        ar.pop()

        B_uT = [Buf("uT%d" % j) for j in range(12)]
        if mode in ("full", "hy_test"):
            ar.push()
            cw = ar.alloc(36, F32).rearrange("p (j k) -> p j k", k=3)
            cb = ar.alloc(12, F32)
            B_cw = Buf("cw")
            dma(sp, cw, cw_d, (), (B_cw,))
            dma(sp, cb, cb_d, (), (B_cw,))
            Wh = [ar.alloc(8 * 128, BF16).rearrange("p (k n) -> p k n", k=8) for _ in range(2)]
            B_Wh = [Buf("Wh0"), Buf("Wh1")]
            raws = [ar.alloc(L + 2, F32) for _ in range(2)]
            B_raws = [Buf("raw0"), Buf("raw1")]
            acc = ar.alloc(L, F32)
            B_acc = Buf("acc")
            outb = [ar.alloc(L, BF16) for _ in range(2)]
            B_outb = [Buf("outb0"), Buf("outb1")]
            for rr_ in range(2):
                memset(pool, raws[rr_][:, 0:1], 0.0, (B_raws[rr_],))
                memset(pool, raws[rr_][:, L + 1:L + 2], 0.0, (B_raws[rr_],))
            for j in range(12):
                s = j % 2
                raw = raws[s]
                B_raw = B_raws[s]
                dma(pool, Wh[s], winv[:, :, 1536 + j * 128:1536 + (j + 1) * 128], (), (B_Wh[s],))
                for tb in range(8):
                    pb = tb % 2
                    for kc in range(8):
                        mm(banks[pb][:, :], Wh[s][:, kc, :], hT[:, kc, tb * 512:(tb + 1) * 512], kc == 0, kc == 7,
                           [B_Wh[s]] + B_hT[tb * 4:tb * 4 + 4], (PB[pb],))
                    actf(raw[:, 1 + tb * 512:1 + (tb + 1) * 512], banks[pb][:, :], AF.Copy, (PB[pb],), (B_raw,))
                ts(dve, acc, raw[:, 1:L + 1], cw[:, j, 1:2], cb[:, j:j + 1], ALU.mult, ALU.add, (B_raw, B_cw), (B_acc,))
                stt(dve, acc, raw[:, 0:L], cw[:, j, 0:1], acc, ALU.mult, ALU.add, (B_raw, B_cw, B_acc), (B_acc,))
                stt(dve, outb[s], raw[:, 2:L + 2], cw[:, j, 2:3], acc, ALU.mult, ALU.add, (B_raw, B_cw, B_acc), (B_outb[s],))
                dma(sp, uT_d[j * 128:(j + 1) * 128, :], outb[s], (B_outb[s],), (B_uT[j],))
            ar.pop()

        if mode in ("full", "na_test"):
            rcos = ar.alloc(L, BF16)
            rsin = ar.alloc(L, BF16)
            biasT2 = ar.alloc(8 * 15 * 64, BF16)
            biasT = biasT2.rearrange("p (h r c) -> p h r c", h=8, r=15)
            B_tab = Buf("natab")
            dma(sp, rcos, rcos_d, (), (B_tab,))
            dma(sp, rsin, rsin_d, (), (B_tab,))
            dma(pool, biasT2, bias_d, (), (B_tab,))
            c8 = ar.alloc(1, F32)
            B_c8 = Buf("c8")
            memset(pool, c8, 0.125, (B_c8,))
            qrT = ar.alloc(L, BF16)
            qpT = ar.alloc(L, BF16)
            krT = ar.alloc(L, BF16)
            B_q = [Buf("q%d" % i) for i in range(8)]
            Vaug = ar.alloc(NT * 2 * 65, BF16).rearrange("p (t h d) -> p t h d", t=NT, h=2)
            B_V = [Buf("V%d" % i) for i in range(NT)]
            Vc = ar.alloc(2 * 2 * 65, BF16).rearrange("p (t h d) -> p t h d", t=2, h=2)
            B_Vc = Buf("Vc")
            kcT = ar.alloc(CTX, BF16)
            B_kc = Buf("kcT")
            wqkp_v = wqkp_d.rearrange("(k p) n -> p k n", p=128)
            Wp_all = [[ar.alloc(8 * 128, BF16).rearrange("p (k n) -> p k n", k=8) for _ in range(5)] for _ in range(2)]
            B_Wp_all = [Buf("Wp0"), Buf("Wp1")]

            def load_Wp(pr_):
                c0_ = pr_ * 128
                W_, B_ = Wp_all[pr_ % 2], B_Wp_all[pr_ % 2]
                dma(pool, W_[0], winv[:, :, c0_:c0_ + 128], (), (B_,))
                dma(pool, W_[1], wqkp_v[:, :, c0_:c0_ + 128], (), (B_,))
                dma(pool, W_[2], winv[:, :, 512 + c0_:512 + c0_ + 128], (), (B_,))
                dma(pool, W_[3], wqkp_v[:, :, 512 + c0_:512 + c0_ + 128], (), (B_,))
                dma(pool, W_[4], winv[:, :, 1024 + c0_:1024 + c0_ + 128], (), (B_,))
            t1 = [ar.alloc(512, F32) for _ in range(2)]
            t2 = [ar.alloc(512, F32) for _ in range(2)]
            B_t1 = [Buf("t1_0"), Buf("t1_1")]
            B_t2 = [Buf("t2_0"), Buf("t2_1")]
            PT = [ar.alloc(7 * 64, BF16) for _ in range(4)]
            B_PT = [Buf("PT%d" % i) for i in range(4)]
            rec = [ar.alloc(2, F32) for _ in range(2)]
            B_rec = [Buf("rec0"), Buf("rec1")]
            ynorm = [ar.alloc(128, BF16) for _ in range(2)]
            B_yn = [Buf("yn0"), Buf("yn1")]
            memset(pool, Vaug[:, :, :, 64:65], 1.0, B_V)
            memset(pool, Vc[:, :, :, 64:65], 1.0, (B_Vc,))
            sidx = 0
            import os as _os
            _NP = int(_os.environ.get('NA_PAIRS', '4'))
            _NTT = int(_os.environ.get('NA_NT', str(NT)))
            _STG = int(_os.environ.get('NA_STAGE', '9'))
            for pr in range(_NP):
                c0 = pr * 128
                Wp, B_Wp = Wp_all[pr % 2], B_Wp_all[pr % 2]
                if pr == 0:
                    load_Wp(0)
                if pr + 1 < _NP:
                    load_Wp(pr + 1)
                for kc in range(8 if _STG >= 1 else 0):
                    mm(banks[6][:, 0:CTX], Wp[2][:, kc, :], hcT[:, kc, :], kc == 0, kc == 7, (B_Wp, B_hcT), (PB[6],))
                if _STG >= 1:
                    actf(kcT, banks[6][:, 0:CTX], AF.Copy, (PB[6],), (B_kc,))
                for ct in range(2 if _STG >= 2 else 0):
                    for kc in range(8):
                        mm(banks[7][:, ct * 128:(ct + 1) * 128], hcT[:, kc, ct * 128:(ct + 1) * 128], Wp[4][:, kc, :],
                           kc == 0, kc == 7, (B_Wp, B_hcT), (PB[7],))
                if _STG >= 2:
                    cp(dve, Vc[:, :, :, 0:64], banks[7][:, 0:256].rearrange("p (t h d) -> p t h d", t=2, h=2),
                       (PB[7],), (B_Vc,))
                for tb in range(8 if _STG >= 3 else 0):
                    tsl = slice(tb * 512, (tb + 1) * 512)
                    RH = [B_Wp] + B_hT[tb * 4:tb * 4 + 4]
                    for wi in range(4):
                        for kc in range(8):
                            mm(banks[wi][:, :], Wp[wi][:, kc, :], hT[:, kc, tsl], kc == 0, kc == 7, RH, (PB[wi],))
                    s = tb % 2
                    _SUB = int(_os.environ.get('NA_SUB', '9'))
                    if _SUB >= 1:
                        tt(dve, t1[s], banks[0][:, :], rcos[:, tsl], ALU.mult, (PB[0], B_tab), (B_t1[s],))
                    if _SUB >= 2:
                        actf(qpT[:, tsl], banks[0][:, :], AF.Identity, (PB[0], B_c8), (B_q[tb],), scale=c8[:, 0:1])
                    if _SUB >= 3:
                        stt(dve, t2[s], banks[1][:, :], 0.125, rsin[:, tsl], ALU.mult, ALU.mult, (PB[1], B_tab), (B_t2[s],))
                    if _SUB >= 4:
                        stt(dve, qrT[:, tsl], t1[s], 0.125, t2[s], ALU.mult, ALU.add, (B_t1[s], B_t2[s]), (B_q[tb],))
                    if _SUB >= 5:
                        tt(dve, t1[s], banks[2][:, :], rcos[:, tsl], ALU.mult, (PB[2], B_tab), (B_t1[s],))
                        tt(dve, t2[s], banks[3][:, :], rsin[:, tsl], ALU.mult, (PB[3], B_tab), (B_t2[s],))
                    if _SUB >= 6:
                        tt(pool, krT[:, tsl], t1[s], t2[s], ALU.add, (B_t1[s], B_t2[s]), (B_q[tb],))
                for ti in range(NT if _STG >= 4 else 0):
                    pb = 4 + (ti % 2)
                    for kc in range(8):
                        mm(banks[pb][:, 0:128], hT[:, kc, ti * 128:(ti + 1) * 128], Wp[4][:, kc, :], kc == 0, kc == 7,
                           (B_Wp, B_hT[ti]), (PB[pb],))
                    actf(Vaug[:, ti, :, 0:64], banks[pb][:, 0:128].rearrange("p (h d) -> p h d", h=2), AF.Copy,
                         (PB[pb],), (B_V[ti],))
                def qk_list(ti, rr, hl, sb):
                    r = 2 * ti + rr
                    row0 = min(max(r - 4, 0), 56)
                    if row0 % 2 == 0:
                        tiles = [("p", row0 + 2 * j) for j in range(4)]
                    else:
                        tiles = [("o", row0)] + [("p", row0 + 1 + 2 * j) for j in range(3)] + [("e", row0 + 7)]
                    qs = slice(r * 64, (r + 1) * 64)
                    h = pr * 2 + hl
                    hp = slice(hl * 64, (hl + 1) * 64)
                    S = banks[sb]
                    RK = [B_q[tb] for tb in sorted(set([(row0 * 64) // 512, (row0 * 64 + 511) // 512, (r * 64) // 512]))]
                    ops = []
                    for ct in range(2):
                        ops.append((S[:, ct * 64:(ct + 1) * 64], kcT[hp, ct * 128:(ct + 1) * 128], qpT[hp, qs], True, True,
                                    [B_kc] + RK, (PB[sb],)))
                    for si, (kind, ka) in enumerate(tiles):
                        sl_ = slice((2 + si) * 64, (3 + si) * 64)
                        roff = ka - r + 7
                        if kind == "p":
                            o_ap = S[:, sl_]
                            k_ap = krT[hp, ka * 64:ka * 64 + 128]
                            b_ap = biasT[hp, h, roff:roff + 2, :].rearrange("p r c -> p (r c)")
                        elif kind == "o":
                            o_ap = S[64:128, sl_]
                            k_ap = krT[hp, ka * 64:ka * 64 + 64]
                            b_ap = biasT[hp, h, roff, :]
                        else:
                            o_ap = S[0:64, sl_]
                            k_ap = krT[hp, ka * 64:ka * 64 + 64]
                            b_ap = biasT[hp, h, roff, :]
                        ops.append((o_ap, k_ap, qrT[hp, qs], True, False, RK, (PB[sb],)))
                        ops.append((o_ap, b_ap, ident[hp, hp], False, True, (B_tab, B_const), (PB[sb],)))
                    return ops, tiles

                def do_exp(sb, tiles):
                    ns = 2 + len(tiles)
                    actf(PT[sb][:, 0:ns * 64], banks[sb][:, 0:ns * 64], AF.Exp, (PB[sb],), (B_PT[sb],))

                def pv(ti, rr, hl, sb, tiles):
                    ob = 4 + (ti % 2)
                    O = banks[ob][rr * 64:(rr + 1) * 64, hl * 65:(hl + 1) * 65]
                    for ct in range(2):
                        mm(O, PT[sb][:, ct * 64:(ct + 1) * 64], Vc[:, ct, hl, :], ct == 0, False,
                           (B_PT[sb], B_Vc), (PB[ob],))
                    for si, (kind, ka) in enumerate(tiles):
                        sl_ = slice((2 + si) * 64, (3 + si) * 64)
                        last = si == len(tiles) - 1
                        if kind == "p":
                            mm(O, PT[sb][:, sl_], Vaug[:, ka // 2, hl, :], False, last, (B_PT[sb], B_V[ka // 2]), (PB[ob],))
                        elif kind == "o":
                            mm(O, PT[sb][64:128, sl_], Vaug[64:128, ka // 2, hl, :], False, last,
                               (B_PT[sb], B_V[ka // 2]), (PB[ob],))
                        else:
                            mm(O, PT[sb][0:64, sl_], Vaug[0:64, ka // 2, hl, :], False, last,
                               (B_PT[sb], B_V[ka // 2]), (PB[ob],))

                def finish(ti):
                    ob = 4 + (ti % 2)
                    s = ti % 2
                    Ov = banks[ob][:, 0:130].rearrange("p (h d) -> p h d", h=2)
                    dve.op(lambda h_, o=rec[s], i=Ov[:, :, 64]: h_.reciprocal(out=o, in_=i), (PB[ob],), (B_rec[s],))
                    for hl in range(2):
                        ts(dve, ynorm[s][:, hl * 64:(hl + 1) * 64], Ov[:, hl, 0:64], rec[s][:, hl:hl + 1], None, ALU.mult, None,
                           (PB[ob], B_rec[s]), (B_yn[s],))
                    tb_ = 6 + s
                    pbv = banks[tb_][:, :].bitcast(BF16)
                    tr(pbv[:, 0:128], ynorm[s], ident, (B_yn[s], B_const), (PB[tb_],))
                    actf(ynaP[:, ti * 128:(ti + 1) * 128], pbv[:, 0:128], AF.Copy, (PB[tb_],), (B_yna,))

                rows_ = [(ti, rr) for ti in range(_NTT) for rr in range(2)]
                pend = None
                for idx, (ti, rr) in enumerate(rows_):
                    sb0 = (idx % 2) * 2
                    ops0, tiles = qk_list(ti, rr, 0, sb0)
                    ops1, _ = qk_list(ti, rr, 1, sb0 + 1)
                    for oa, ob_ in zip(ops0, ops1):
                        mm(*oa)
                        mm(*ob_)
                    do_exp(sb0, tiles)
                    do_exp(sb0 + 1, tiles)
                    if pend is not None:
                        pti, prr, psb, ptiles = pend
                        pv(pti, prr, 0, psb, ptiles)
                        pv(pti, prr, 1, psb + 1, ptiles)
                        if prr == 1:
                            finish(pti)
                    pend = (ti, rr, sb0, tiles)
                if pend is not None:
                    pti, prr, psb, ptiles = pend
                    pv(pti, prr, 0, psb, ptiles)
                    pv(pti, prr, 1, psb + 1, ptiles)
                    finish(pti)
                dma(sp, yna_v[:, pr, :], ynaP, (B_yna,), (B_ynad,))
                if dbg and pr == 0:
                    dma(sp, dbg2_d, ynaP, (B_yna,), (B_ynad,))
        else:
            memset(pool, ynaP, 0.0, (B_yna,))
            for c4 in range(4):
                dma(sp, yna_v[:, c4, :], ynaP, (B_yna,), (B_ynad,))
        if dbg:
            pass
        ar.pop()

        if mode in ("full", "hy_test"):
            ar.push()
            Ere2 = ar.alloc(64 * 128, BF16)
            Eim2 = ar.alloc(64 * 128, BF16)
            Ere = Ere2.rearrange("p (k f) -> p k f", k=64)
            Eim = Eim2.rearrange("p (k f) -> p k f", k=64)
            T2 = ar.alloc(2 * 128 * 32, BF16)
            Tt = T2.rearrange("p (m a b) -> p m a b", m=2, a=128)
            Gt = ar.alloc(512, BF16)
            D1 = ar.alloc(192, BF16)
            hb = ar.alloc(16, F32)
            w3b = ar.alloc(2048, BF16)
            hidT = ar.alloc(8192, BF16)
            B_hyc = Buf("hyconst")
            dma(sp, Ere2, hyEre_d, (), (B_hyc,))
            dma(sp, Eim2, hyEim_d, (), (B_hyc,))
            dma(sp, T2, hyT_d, (), (B_hyc,))
            dma(sp, Gt, hyG_d, (), (B_hyc,))
            dma(sp, D1[0:64, :], hyD1_d, (), (B_hyc,))
            dma(sp, hb[0:64, :], hyb_d, (), (B_hyc,))
            hbrow = ar.alloc(1024, F32)
            dma(sp, hbrow[0:1, :], hybr_d, (), (B_hyc,))
            dma(pool, w3b[0:64, :], hyw3_d, (), (B_hyc,))
            ar.push()
            zT = ar.alloc(8192, F32)
            w1s = ar.alloc(64, F32)
            w2s = ar.alloc(64, F32)
            vecs = ar.alloc(8, F32)
            B_mlp = Buf("mlpc")
            dma(sp, zT[0:33, :], hyz_d, (), (B_mlp,))
            dma(sp, w1s[0:33, :], hyw1_d, (), (B_mlp,))
            dma(sp, w2s[0:64, :], hyw2_d, (), (B_mlp,))
            dma(sp, vecs[0:64, 0:3], hyv_d, (), (B_mlp,))
            ts(dve, vecs[0:64, 3:4], vecs[0:64, 2:3], 1.0 / 3.0, None, ALU.mult, None, (B_mlp,), (B_mlp,))
            tt(dve, vecs[0:64, 4:5], vecs[0:64, 3:4], vecs[0:64, 0:1], ALU.mult, (B_mlp,), (B_mlp,))
            tt(dve, vecs[0:64, 5:6], vecs[0:64, 3:4], vecs[0:64, 1:2], ALU.mult, (B_mlp,), (B_mlp,))
            s1 = [ar.alloc(512, F32) for _ in range(4)]
            sq = [ar.alloc(512, F32) for _ in range(4)]
            h1 = [ar.alloc(512, F32) for _ in range(4)]
            B_s1 = [Buf("s1_%d" % i) for i in range(4)]
            B_sq = [Buf("sq_%d" % i) for i in range(4)]
            B_h1 = [Buf("h1_%d" % i) for i in range(4)]
            B_hid = Buf("hid")
            for blk in range(16):
                s_ = blk % 4
                bsl = slice(blk * 512, (blk + 1) * 512)
                pa, pb2 = 2 * s_, 2 * s_ + 1
                mm(banks[pa][0:64, :], w1s[0:33, :], zT[0:33, bsl], True, True, (B_mlp,), (PB[pa],))
                actf(s1[s_][0:64, :], banks[pa][0:64, :], AF.Sin, (PB[pa], B_mlp), (B_s1[s_],),
                     bias=vecs[0:64, 4:5], scale=vecs[0:64, 3:4])
                tt(dve, sq[s_][0:64, :], s1[s_][0:64, :], s1[s_][0:64, :], ALU.mult, (B_s1[s_],), (B_sq[s_],))
                ts(dve, sq[s_][0:64, :], sq[s_][0:64, :], -4.0, 3.0, ALU.mult, ALU.add, (B_sq[s_],), (B_sq[s_],))
                tt(dve, h1[s_][0:64, :], sq[s_][0:64, :], s1[s_][0:64, :], ALU.mult, (B_sq[s_], B_s1[s_]), (B_h1[s_],))
                mm(banks[pb2][0:64, :], w2s[0:64, :], h1[s_][0:64, :], True, True, (B_mlp, B_h1[s_]), (PB[pb2],))
                actf(s1[s_][0:64, :], banks[pb2][0:64, :], AF.Sin, (PB[pb2], B_mlp), (B_s1[s_],),
                     bias=vecs[0:64, 5:6], scale=vecs[0:64, 3:4])
                tt(dve, sq[s_][0:64, :], s1[s_][0:64, :], s1[s_][0:64, :], ALU.mult, (B_s1[s_],), (B_sq[s_],))
                ts(dve, sq[s_][0:64, :], sq[s_][0:64, :], -4.0, 3.0, ALU.mult, ALU.add, (B_sq[s_],), (B_sq[s_],))
                tt(dve, hidT[0:64, bsl], sq[s_][0:64, :], s1[s_][0:64, :], ALU.mult, (B_sq[s_], B_s1[s_]), (B_hid,))
            memset(pool, hidT[0:64, 4096:4097], 0.0, (B_hid,))
            ar.pop()
            zt = ar.alloc(L, BF16)
            zp = ar.alloc(L, BF16)
            B_zp = Buf("zp")
            x1t_ = ar.alloc(L, BF16)
            x2t_ = ar.alloc(L, BF16)
            B_zt = Buf("zt")
            B_gx1 = Buf("gx1")
            B_gx2 = Buf("gx2")
            decp = [ar.alloc(512, BF16) for _ in range(2)]
            B_decp = [Buf("decp0"), Buf("decp1")]
            NK4 = 9
            NJ = NK4 * 4 * 3
            KI2 = NK4 * 4
            XY = ar.alloc(128 * 64, BF16)
            B_XY = Buf("XY")
            Xv = XY.rearrange("p (f c) -> p f c", c=64)
            Yv5 = XY.rearrange("p (m h l k) -> p m h l k", m=2, h=32, l=2)
            AB = ar.alloc(256 * 32, BF16)
            B_AB = Buf("AB")
            Av3 = AB[:, 0:NJ * 64].rearrange("p (j c) -> p j c", c=64)
            Av4 = AB[:, 0:NJ * 64].rearrange("p (k m c) -> p k m c", m=3, c=64)
            Bv3 = AB.rearrange("p (j q) -> p j q", q=32)
            Bv4 = AB.rearrange("p (m a q) -> p m a q", m=2, a=128)
            Xk = ar.alloc(128 * 64, BF16)
            B_Xk = Buf("Xk")
            Xkv = Xk.rearrange("p (f c) -> p f c", c=64)
            Ak = ar.alloc(NJ * 64, BF16)
            B_Ak = Buf("Ak")
            Akv3 = Ak.rearrange("p (j c) -> p j c", c=64)
            Akv4 = Ak.rearrange("p (k m c) -> p k m c", m=3, c=64)
            Ksp = ar.alloc(NK4 * 512, BF16)
            B_Ksp = Buf("Ksp")
            Kv = Ksp.rearrange("p (k m c) -> p k m c", m=2, c=64)
            tA = [ar.alloc(512, F32) for _ in range(2)]
            tB = [ar.alloc(512, F32) for _ in range(2)]
            B_tA = [Buf("tA0"), Buf("tA1")]
            B_tB = [Buf("tB0"), Buf("tB1")]
            tmpe = [ar.alloc(512, F32) for _ in range(2)]
            B_tmpe = [Buf("tmpe0"), Buf("tmpe1")]
            import os as _os2
            _NG = int(_os2.environ.get('HY_GROUPS', '8'))
            evc = [0]

            def evac(dst, src, R, W):
                actf(dst, src, AF.Copy, R, W)

            def gen_F1(Xsrc, B_X, K, Adst3, B_A, bank0):
                for cp_ in range(16):
                    pb = bank0 + cp_ % 2
                    for i in range(4):
                        c_ = cp_ * 4 + i
                        mm(banks[pb][:, i * NJ:(i + 1) * NJ], Xsrc[0:K, :, c_], D1[0:K, 0:NJ], True, True,
                           (B_X, B_hyc), (PB[pb],))
                    evac(Adst3[:, 0:NJ, cp_ * 4:cp_ * 4 + 4], banks[pb][:, 0:4 * NJ].rearrange("p (c j) -> p j c", c=4),
                         (PB[pb],), (B_A,))
                    yield

            def F2_mm(pb, k4, A4, B_A):
                for i in range(4):
                    k_s = k4 * 4 + i
                    o_ap = banks[pb][:, i * 128:(i + 1) * 128]
                    mm(o_ap, Ere[:, k_s, :], A4[:, k_s, 1:3, :].rearrange("p m c -> p (m c)"), True, False,
                       (B_hyc, B_A), (PB[pb],))
                    mm(o_ap, Eim[:, k_s, :], A4[:, k_s, 0:2, :].rearrange("p m c -> p (m c)"), False, True,
                       (B_hyc, B_A), (PB[pb],))

            def gen_filter12(o, g):
                gc0 = g * 64
                for nf8 in range(16):
                    ds_ = nf8 % 2
                    dma(sp, decp[ds_][0:64, :], hydec_d[g, :, nf8 * 512:(nf8 + 1) * 512], (), (B_decp[ds_],))
                    pb = nf8 % 2
                    for i in range(8):
                        n_f = nf8 * 8 + i
                        wf = (0 * 2 + o) * 512 + gc0
                        wb = (1 * 2 + o) * 512 + gc0
                        mm(banks[pb][0:32, i * 64:(i + 1) * 64], hidT[0:64, n_f:4096:128], w3b[0:64, wf:wf + 64],
                           True, True, (B_hid, B_hyc), (PB[pb],))
                        mm(banks[pb][32:64, i * 64:(i + 1) * 64], hidT[0:64, 4096 + n_f:8192:128], w3b[0:64, wb:wb + 64],
                           True, True, (B_hid, B_hyc), (PB[pb],))
                    tt(dve, Xk[0:64, nf8 * 512:(nf8 + 1) * 512], banks[pb][0:64, :], decp[ds_][0:64, :], ALU.mult,
                       (PB[pb], B_decp[ds_]), (B_Xk,))
                    if nf8 == 0:
                        tt(dve, Xk[0:1, 0:64], Xk[0:1, 0:64], hbrow[0:1, o * 512 + gc0:o * 512 + gc0 + 64], ALU.add,
                           (B_Xk, B_hyc), (B_Xk,))
                    yield
                yield from gen_F1(Xkv, B_Xk, 64, Akv3, B_Ak, 0)

            def gen_filter3(o, g):
                for k4 in range(NK4):
                    pb = k4 % 2
                    F2_mm(pb, k4, Akv4, B_Ak)
                    evac(Ksp[:, k4 * 512:(k4 + 1) * 512], banks[pb][:, :], (PB[pb],), (B_Ksp,))
                    yield

            def gen_data1(o, g):
                gc0 = g * 64
                if o == 0:
                    if g == 0:
                        dma(sp, zt[0:64, :], uT_d[gc0:gc0 + 64, :], (B_uT[gc0 // 128],), (B_zt,))
                elif g + 1 < _NG:
                    gn0 = (g + 1) * 64
                    dma(sp, zt[0:64, :], uT_d[gn0:gn0 + 64, :], (B_uT[gn0 // 128],), (B_zt,))
                for n16 in range(8):
                    pb = 2 + n16 % 2
                    pbv = banks[pb][:, :].bitcast(BF16)
                    for i in range(16):
                        n_f = n16 * 16 + i
                        if o == 0:
                            tr(pbv[0:32, i * 64:(i + 1) * 64], zt[0:64, n_f:4096:128], ident[0:64, 0:64],
                               (B_zt, B_const), (PB[pb],))
                        else:
                            tr(pbv[0:32, i * 64:(i + 1) * 64], zp[0:64, n_f * 32:(n_f + 1) * 32], ident[0:64, 0:64],
                               (B_zp, B_const), (PB[pb],))
                    evac(XY[0:32, n16 * 1024:(n16 + 1) * 1024], pbv[0:32, :], (PB[pb],), (B_XY,))
                    yield
                yield from gen_F1(Xv, B_XY, 32, Av3, B_AB, 2)
                for k4 in range(NK4):
                    pb = 2 + k4 % 2
                    F2_mm(pb, k4, Av4, B_AB)
                    s_ = k4 % 2
                    Uv = banks[pb][:, :].rearrange("p (k m c) -> p k m c", k=4, m=2)
                    Kq = Kv[:, k4 * 4:(k4 + 1) * 4, :, :]
                    tAv = tA[s_].rearrange("p (k m c) -> p k m c", k=4, m=2)
                    tBv = tB[s_].rearrange("p (k m c) -> p k m c", k=4, m=2)
                    tt(dve, tA[s_], banks[pb][:, :], Ksp[:, k4 * 512:(k4 + 1) * 512], ALU.mult,
                       (PB[pb], B_Ksp), (B_tA[s_],))
                    tt(dve, tBv[:, :, 0, :], Uv[:, :, 0, :], Kq[:, :, 1, :], ALU.mult, (PB[pb], B_Ksp), (B_tB[s_],))
                    tt(dve, tBv[:, :, 1, :], Uv[:, :, 1, :], Kq[:, :, 0, :], ALU.mult, (PB[pb], B_Ksp), (B_tB[s_],))
                    yre = Yv5[:, 0, :, :, k4 * 4:(k4 + 1) * 4].rearrange("p h l k -> p l h k")
                    yim = Yv5[:, 1, :, :, k4 * 4:(k4 + 1) * 4].rearrange("p h l k -> p l h k")
                    pm = lambda ap_: ap_.rearrange("p k (l h) -> p l h k", l=2)
                    tt(pool, yre, pm(tAv[:, :, 0, :]), pm(tAv[:, :, 1, :]), ALU.subtract, (B_tA[s_],), (B_XY,))
                    tt(pool, yim, pm(tBv[:, :, 0, :]), pm(tBv[:, :, 1, :]), ALU.add, (B_tB[s_],), (B_XY,))
                    yield

            def gen_data2(o, g):
                gc0 = g * 64
                gate = x1t_ if o == 0 else x2t_
                B_gate = B_gx1 if o == 0 else B_gx2
                if o == 0:
                    dma(sp, x1t_[0:64, :], uT_d[512 + gc0:512 + gc0 + 64, :], (B_uT[(512 + gc0) // 128],), (B_gx1,))
                    dma(sp, x2t_[0:64, :], uT_d[1024 + gc0:1024 + gc0 + 64, :], (B_uT[(1024 + gc0) // 128],), (B_gx2,))
                for q2 in range(16):
                    pb = 2 + q2 % 2
                    for i in range(2):
                        p_ = q2 * 2 + i
                        o_ap = banks[pb][:, i * 256:(i + 1) * 256]
                        mm(o_ap, Yv5[:, 0, p_, :, :].rearrange("p l k -> p (l k)"), Gt[:, 0:256], True, False,
                           (B_XY, B_hyc), (PB[pb],))
                        mm(o_ap, Yv5[:, 1, p_, :, :].rearrange("p l k -> p (l k)"), Gt[:, 256:512], False, True,
                           (B_XY, B_hyc), (PB[pb],))
                    evac(Bv3[:, :, q2 * 2:q2 * 2 + 2], banks[pb][:, :].rearrange("p (q j) -> p j q", q=2), (PB[pb],), (B_AB,))
                    yield
                ztv = zt.rearrange("p (b a) -> p a b", a=128)
                gtv = gate.rearrange("p (b a) -> p a b", a=128)
                for qq in range(4):
                    for cl in range(2):
                        kp = slice(cl * 64, cl * 64 + KI2)
                        op_ = slice(cl * 32, (cl + 1) * 32)
                        for al in range(32):
                            n_a = qq * 32 + al
                            bk = 4 + 2 * cl + (al // 16)
                            o_ap = banks[bk][op_, (al % 16) * 32:(al % 16 + 1) * 32]
                            mm(o_ap, Bv4[kp, 0, n_a, :], Tt[kp, 0, n_a, :], True, False, (B_AB, B_hyc), (PB[bk],))
                            mm(o_ap, Bv4[kp, 1, n_a, :], Tt[kp, 1, n_a, :], False, True, (B_AB, B_hyc), (PB[bk],))
                        yield
                    for cl in range(2):
                        op_ = slice(cl * 32, (cl + 1) * 32)
                        for hb_ in range(2):
                            bk = 4 + 2 * cl + hb_
                            a0 = qq * 32 + hb_ * 16
                            s_ = (cl * 2 + hb_) % 2
                            zpv = zp[op_, a0 * 32:(a0 + 16) * 32].rearrange("p (a b) -> p a b", b=32)
                            tt(dve, zpv, banks[bk][op_, :].rearrange("p (a b) -> p a b", b=32), gtv[op_, a0:a0 + 16, :],
                               ALU.mult, (PB[bk], B_gate), (B_zp,))
                        yield
                if o == 1:
                    zpu = zp.rearrange("p (a b) -> p b a", b=32)
                    ztu = x1t_.rearrange("p (b a) -> p b a", a=128)
                    for hh in range(2):
                        actf(ztu[0:64, hh * 16:(hh + 1) * 16, :], zpu[0:64, hh * 16:(hh + 1) * 16, :], AF.Copy, (B_zp,), (B_gx1,))
                    dma(sp, yhy_d[gc0:gc0 + 64, :], x1t_[0:64, :], (B_gx1,), (B_yhyd[g],))
                    if dbg and g == 0:
                        dma(sp, dbg3_d, x1t_[0:64, :], (B_gx1,), (B_yhyd[g],))
                        dma(sp, dbg4_d, hidT[0:64, :], (B_hid,), (B_yhyd[g],))
                    yield

            def interleave(ga, gb):
                gens = [x for x in (ga, gb) if x is not None]
                while gens:
                    for gen_ in list(gens):
                        try:
                            next(gen_)
                        except StopIteration:
                            gens.remove(gen_)

            items = [(o, g) for g in range(_NG) for o in range(2)]
            for g in range(_NG, 8):
                pass
            interleave(gen_filter12(*items[0]), None)
            interleave(gen_filter3(*items[0]), None)
            for ii, it in enumerate(items):
                nxt = items[ii + 1] if ii + 1 < len(items) else None
                interleave(gen_data1(*it), gen_filter12(*nxt) if nxt else None)
                interleave(gen_data2(*it), gen_filter3(*nxt) if nxt else None)
            for g in range(_NG, 8):
                dma(sp, yhy_d[g * 64:(g + 1) * 64, :], zt[0:64, :], (B_zt,), (B_yhyd[g],))
            ar.pop()
        else:
            ar.push()
            zt = ar.alloc(L, BF16)
            B_zt = Buf("zt")
            memset(pool, zt, 0.0, (B_zt,))
            for g in range(8):
                dma(sp, yhy_d[g * 64:(g + 1) * 64, :], zt[0:64, :], (B_zt,), (B_yhyd[g],))
            ar.pop()

        ar.push()
        Wg = ar.alloc(8 * 2048, BF16).rearrange("p (k n) -> p k n", k=8)
        Wnao = ar.alloc(4 * D, BF16).rearrange("p (k n) -> p k n", k=4)
        Whyo = ar.alloc(4 * D, BF16).rearrange("p (k n) -> p k n", k=4)
        Wout = ar.alloc(8 * D, BF16).rearrange("p (k n) -> p k n", k=8)
        B_wg = [Buf("wg%d" % i) for i in range(4)]
        B_wno = Buf("wno")
        B_who = Buf("who")
        B_wo = [Buf("wo0"), Buf("wo1")]
        winv = win_d.rearrange("(k p) n -> p k n", p=128)
        for hh in (0, 2):
            dma(pool, Wg[:, :, hh * 512:(hh + 1) * 512], winv[:, :, 3072 + hh * 512:3072 + (hh + 1) * 512], (), (B_wg[hh],))
        dma(pool, Wnao, wnao_d.rearrange("(k p) n -> p k n", p=128), (), (B_wno,))
        dma(pool, Whyo, whyo_d.rearrange("(k p) n -> p k n", p=128), (), (B_who,))
        for hh in (1, 3):
            dma(pool, Wg[:, :, hh * 512:(hh + 1) * 512], winv[:, :, 3072 + hh * 512:3072 + (hh + 1) * 512], (), (B_wg[hh],))
        for hh in range(2):
            dma(pool, Wout[:, :, hh * 512:(hh + 1) * 512], wout_d.rearrange("(k p) n -> p k n", p=128)[:, :, hh * 512:(hh + 1) * 512], (), (B_wo[hh],))
        xt4 = [ar.alloc(D, F32) for _ in range(4)]
        B_xt4 = [Buf("xt4_%d" % i) for i in range(4)]
        xn4 = [ar.alloc(D, BF16) for _ in range(4)]
        B_xn4 = [Buf("xn4_%d" % i) for i in range(4)]
        stat4 = [ar.alloc(4, F32) for _ in range(4)]
        B_stat4 = [Buf("st4_%d" % i) for i in range(4)]
        xt4b = [ar.alloc(D, F32) for _ in range(4)]
        B_xt4b = [Buf("xt4b_%d" % i) for i in range(4)]
        xn2 = [ar.alloc(D, BF16) for _ in range(2)]
        B_xn2 = [Buf("xn2_%d" % i) for i in range(2)]
        junk = ar.alloc(D, BF16)
        B_junk = Buf("junk")
        stat = [ar.alloc(4, F32) for _ in range(2)]
        B_stat = [Buf("st0"), Buf("st1")]
        hTc = ar.alloc(8 * 512, BF16).rearrange("p (k t) -> p k t", k=8)
        B_hTc = Buf("hTc")
        hTcb = ar.alloc(8 * 512, BF16).rearrange("p (k t) -> p k t", k=8)
        B_hTcb = Buf("hTcb")
        sg = ar.alloc(4 * 512, BF16).rearrange("p (k t) -> p k t", k=4)
        B_sg = [Buf("sg0"), Buf("sg1")]
        mT = ar.alloc(8 * 512, BF16).rearrange("p (k t) -> p k t", k=8)
        B_mT = Buf("mT")
        tmpa = [ar.alloc(512, F32) for _ in range(2)]
        B_tmpa = [Buf("tmpa0"), Buf("tmpa1")]
        tmpb = [ar.alloc(512, F32) for _ in range(2)]
        B_tmpb = [Buf("tmpb0"), Buf("tmpb1")]
        x1t = [ar.alloc(D, F32) for _ in range(2)]
        B_x1t = [Buf("x1t0"), Buf("x1t1")]
        ynac = [ar.alloc(4 * 512, BF16).rearrange("p (c t) -> p c t", c=4) for _ in range(2)]
        yhyc = [ar.alloc(4 * 512, BF16).rearrange("p (c t) -> p c t", c=4) for _ in range(2)]
        B_ynac = [Buf("ynac0"), Buf("ynac1")]
        B_yhyc = [Buf("yhyc0"), Buf("yhyc1")]
        B_x1d = [Buf("x1d%d" % i) for i in range(NT)]
        x_v = x_d.rearrange("(n p) d -> n p d", p=128)
        x1_v = x1_d.rearrange("(n p) d -> n p d", p=128)
        out_v = out_d.rearrange("(n p) d -> n p d", p=128)
        xt4_all = [xt4, xt4b]
        B_xt4_all = [B_xt4, B_xt4b]
        hTc_all = [hTc, hTcb]
        B_hTc_all = [B_hTc, B_hTcb]

        def p4a_load(ch):
            for i in range(4):
                dma(sp, xt4_all[ch % 2][i], x_v[ch * 4 + i], (), (B_xt4_all[ch % 2][i],))
            dma(sp, ynac[ch % 2], yna_v[:, :, ch * 512:(ch + 1) * 512], (B_ynad,), (B_ynac[ch % 2],))
            dma(sp, yhyc[ch % 2], yhy_v[:, :, ch * 512:(ch + 1) * 512], B_yhyd, (B_yhyc[ch % 2],))

        def p4a_norm(ch):
            for i in range(4):
                norm_tile(xt4_all[ch % 2][i], B_xt4_all[ch % 2][i], xn4[i], B_xn4[i], junk, B_junk, stat4[i], B_stat4[i])

        def p4a_tr(ch, i):
            transpose_mod(xn4[i], B_xn4[i], i % 2, hTc_all[ch % 2], i * 128, B_hTc_all[ch % 2], A1, B1)

        p4a_load(0)
        p4a_norm(0)
        for i in range(4):
            p4a_tr(0, i)
        for ch in range(8):
            xt4 = xt4_all[ch % 2]
            B_xt4 = B_xt4_all[ch % 2]
            hTc = hTc_all[ch % 2]
            B_hTc = B_hTc_all[ch % 2]
            cs_ = ch % 2
            if ch + 1 < 8:
                p4a_load(ch + 1)
                p4a_norm(ch + 1)
            if mode == "dense_test":
                cp(pool, ynac[cs_], hTc[:, 0:4, :], (B_hTc,), (B_ynac[cs_],))
                cp(pool, yhyc[cs_], hTc[:, 4:8, :], (B_hTc,), (B_yhyc[cs_],))
            for dc in range(8):
                s = dc % 2
                for gi, j in enumerate((dc, 8 + dc)):
                    pb = (2 + gi) if dc % 2 == 0 else gi
                    for kc in range(8):
                        mm(banks[pb][:, :], Wg[:, kc, j * 128:(j + 1) * 128], hTc[:, kc, :], kc == 0, kc == 7,
                           (B_wg[(j * 128) // 512], B_hTc), (PB[pb],))
                    actf(sg[:, s * 2 + gi, :], banks[pb][:, :], AF.Sigmoid, (PB[pb],), (B_sg[s],))
                pa, pbb = 4 + 2 * (dc % 2), 5 + 2 * (dc % 2)
                for fc in range(4):
                    mm(banks[pa][:, :], Wnao[:, fc, dc * 128:(dc + 1) * 128], ynac[cs_][:, fc, :],
                       fc == 0, fc == 3, (B_wno, B_ynac[cs_]), (PB[pa],))
                for fc in range(4):
                    mm(banks[pbb][:, :], Whyo[:, fc, dc * 128:(dc + 1) * 128], yhyc[cs_][:, fc, :],
                       fc == 0, fc == 3, (B_who, B_yhyc[cs_]), (PB[pbb],))
                tt(dve, tmpa[s], banks[pa][:, :], sg[:, s * 2, :], ALU.mult, (PB[pa], B_sg[s]), (B_tmpa[s],))
                tt(dve, tmpb[s], banks[pbb][:, :], sg[:, s * 2 + 1, :], ALU.mult, (PB[pbb], B_sg[s]), (B_tmpb[s],))
                tt(pool, mT[:, dc, :], tmpa[s], tmpb[s], ALU.add, (B_tmpa[s], B_tmpb[s]), (B_mT,))
            for i in range(4):
                ti = ch * 4 + i
                s = i % 2
                for hh in range(2):
                    pb = 2 + hh
                    for kc in range(8):
                        mm(banks[pb][:, :], mT[:, kc, i * 128:(i + 1) * 128], Wout[:, kc, hh * 512:(hh + 1) * 512],
                           kc == 0, kc == 7, (B_mT, B_wo[hh]), (PB[pb],))
                    tt(dve, x1t[s][:, hh * 512:(hh + 1) * 512], banks[pb][:, :], g1row[:, hh * 512:(hh + 1) * 512],
                       ALU.mult, (PB[pb], B_mod), (B_x1t[s],))
                tt(pool, x1t[s], x1t[s], xt4[i], ALU.add, (B_x1t[s], B_xt4[i]), (B_x1t[s],))
                dma(sp, x1_v[ti], x1t[s], (B_x1t[s],), (B_x1d[ti],))
                if ch + 1 < 8:
                    p4a_tr(ch + 1, i)
        ar.pop()

        ar.push()
        W1 = ar.alloc(8 * 2816, BF16).rearrange("p (k n) -> p k n", k=8)
        W3 = ar.alloc(8 * 2816, BF16).rearrange("p (k n) -> p k n", k=8)
        W2 = ar.alloc(22 * D, BF16).rearrange("p (k n) -> p k n", k=22)
        B_w5 = Buf("w4b")
        w1v = w1_d.rearrange("(k p) n -> p k n", p=128)
        w3v = w3_d.rearrange("(k p) n -> p k n", p=128)
        w2v = w2_d.rearrange("(k p) n -> p k n", p=128)
        B_w1 = [Buf("w1_%d" % i) for i in range(11)]
        B_w3 = [Buf("w3_%d" % i) for i in range(11)]
        B_w2 = [Buf("w2_%d" % i) for i in range(11)]
        for hh in range(11):
            dma(pool, W1[:, :, hh * 256:(hh + 1) * 256], w1v[:, :, hh * 256:(hh + 1) * 256], (), (B_w1[hh],))
            dma(pool, W3[:, :, hh * 256:(hh + 1) * 256], w3v[:, :, hh * 256:(hh + 1) * 256], (), (B_w3[hh],))
        for hh in range(11):
            dma(pool, W2[:, hh * 2:(hh + 1) * 2, :], w2v[:, hh * 2:(hh + 1) * 2, :], (), (B_w2[hh],))
        xb = [ar.alloc(D, F32) for _ in range(4)]
        B_xb = [Buf("xb%d" % i) for i in range(4)]
        xn2 = [ar.alloc(D, BF16) for _ in range(2)]
        B_xn2 = [Buf("xn2b_%d" % i) for i in range(2)]
        junk = ar.alloc(D, BF16)
        B_junk = Buf("junkb")
        stat = [ar.alloc(4, F32) for _ in range(2)]
        B_stat = [Buf("stb0"), Buf("stb1")]
        h2T = ar.alloc(8 * 256, BF16).rearrange("p (k t) -> p k t", k=8)
        B_h2T = Buf("h2T")
        h2Tb = ar.alloc(8 * 256, BF16).rearrange("p (k t) -> p k t", k=8)
        B_h2Tb = Buf("h2Tb")
        aT = ar.alloc(22 * 256, BF16).rearrange("p (k t) -> p k t", k=22)
        B_aT = Buf("aT")
        sl = [ar.alloc(256, F32) for _ in range(2)]
        B_sl = [Buf("sl0"), Buf("sl1")]
        x2 = [ar.alloc(D, F32) for _ in range(2)]
        B_x2 = [Buf("x2_0"), Buf("x2_1")]
        ot = [ar.alloc(D, F32) for _ in range(2)]
        B_ot = [Buf("ot0"), Buf("ot1")]
        B_outd = Buf("outd")
        fst = [ar.alloc(4, F32) for _ in range(2)]
        B_fst = [Buf("fst0"), Buf("fst1")]
        cnt = 0
        h2T_all = [h2T, h2Tb]
        B_h2T_all = [B_h2T, B_h2Tb]

        def p4b_load(ch):
            for i in range(2):
                ti = ch * 2 + i
                bi = (ch % 2) * 2 + i
                dma(sp, xb[bi], x1_v[ti], (B_x1d[ti],), (B_xb[bi],))

        def p4b_norm(ch):
            for i in range(2):
                bi = (ch % 2) * 2 + i
                norm_tile(xb[bi], B_xb[bi], xn2[i], B_xn2[i], junk, B_junk, stat[i], B_stat[i])

        def p4b_tr(ch, i):
            transpose_mod(xn2[i], B_xn2[i], i, h2T_all[ch % 2], i * 128, B_h2T_all[ch % 2], A2, B2)

        p4b_load(0)
        p4b_norm(0)
        for i in range(2):
            p4b_tr(0, i)
        for ch in range(16):
            h2T = h2T_all[ch % 2]
            B_h2T = B_h2T_all[ch % 2]
            if ch + 1 < 16:
                p4b_load(ch + 1)
                p4b_norm(ch + 1)
            for j in range(22):
                pa, pbb = 2 + 2 * (j % 2), 3 + 2 * (j % 2)
                for kc in range(8):
                    mm(banks[pa][:, 0:256], W1[:, kc, j * 128:(j + 1) * 128], h2T[:, kc, :], kc == 0, kc == 7,
                       (B_w1[j // 2], B_h2T), (PB[pa],))
                for kc in range(8):
                    mm(banks[pbb][:, 0:256], W3[:, kc, j * 128:(j + 1) * 128], h2T[:, kc, :], kc == 0, kc == 7,
                       (B_w3[j // 2], B_h2T), (PB[pbb],))
                s = j % 2
                actf(sl[s], banks[pa][:, 0:256], AF.Silu, (PB[pa],), (B_sl[s],))
                tt(dve, aT[:, j, :], banks[pbb][:, 0:256], sl[s], ALU.mult, (PB[pbb], B_sl[s]), (B_aT,))
            for i in range(2):
                ti = ch * 2 + i
                bi = (ch % 2) * 2 + i
                s = cnt % 2
                cnt += 1
                for hh in range(2):
                    pb = 6 + hh
                    for j in range(22):
                        mm(banks[pb][:, :], aT[:, j, i * 128:(i + 1) * 128], W2[:, j, hh * 512:(hh + 1) * 512],
                           j == 0, j == 21, (B_aT, B_w2[j // 2]), (PB[pb],))
                    tt(dve, x2[s][:, hh * 512:(hh + 1) * 512], banks[pb][:, :], g2row[:, hh * 512:(hh + 1) * 512],
                       ALU.mult, (PB[pb], B_mod), (B_x2[s],))
                tt(pool, x2[s], x2[s], xb[bi], ALU.add, (B_x2[s], B_xb[bi]), (B_x2[s],))
                actf(junk, x2[s], AF.Square, (B_x2[s],), (B_junk, B_fst[s]), accum_out=fst[s][:, 0:1])
                rstd_from_ssq(fst[s], B_fst[s])
                stt(dve, ot[s], x2[s], fst[s][:, 2:3], fgrow, ALU.mult, ALU.mult, (B_x2[s], B_fst[s], B_const), (B_ot[s],))
                dma(sp, out_v[ti], ot[s], (B_ot[s],), (B_outd,))
                if ch + 1 < 16:
                    p4b_tr(ch + 1, i)
        ar.pop()

        if dbg:
            dbt = ar.alloc(4096, F32)
            B_dbt = Buf("dbt")
            memset(pool, dbt, 0.0, (B_dbt,))
            cp(dve, dbt[:, 0:96], modT.rearrange("p c j -> p (c j)"), (B_mod,), (B_dbt,))
            cp(dve, dbt[:, 128:128 + 1024], g1row, (B_mod,), (B_dbt,))
            dma(sp, dbg_d, dbt, (B_dbt,), (B_outd,))

        for t in sp.dtoks[-NDMA:]:
            sp.wait(t)

        with nc.Block() as block:
            @block.tensor
            def _(h):
                pe.replay(h)

            @block.scalar
            def _(h):
                act.replay(h)

            @block.vector
            def _(h):
                dve.replay(h)

            @block.gpsimd
            def _(h):
                pool.replay(h)

            @block.sync
            def _(h):
                sp.replay(h)
    print("instr counts: pe %d act %d dve %d pool %d sp %d(dma %d) pooldma %d" % (
        pe.n, act.n, dve.n, pool.n, sp.n, sp.ndma, pool.ndma))
    return nc


_NC = {}


def make_in_maps(inputs):
    hc = host_consts()
    f = lambda a: np.ascontiguousarray(np.asarray(a, dtype=np.float32))
    x = f(inputs["x"])
    c = f(inputs["c"])
    ctx = f(inputs["ctx"])
    c_ctx = f(inputs["c_ctx"])
    shared = {
        "w_ada": f(inputs["w_ada"][0]),
        "b_adaT": f(np.asarray(inputs["b_ada"][0]).reshape(48, 128).T),
        "n1gT": f(np.asarray(inputs["norm1_g"][0]).reshape(8, 128).T),
        "n2gT": f(np.asarray(inputs["norm2_g"][0]).reshape(8, 128).T),
        "fg_row": f(np.broadcast_to(np.asarray(inputs["final_g"]), (128, D))),
        "w_in": f(inputs["w_in"][0]),
        "w_na_o": f(inputs["w_na_o"][0]),
        "w_hy_o": f(inputs["w_hy_o"][0]),
        "w_out": f(inputs["w_out"][0]),
        "ffn_w1": f(inputs["ffn_w1"][0]),
        "ffn_w3": f(inputs["ffn_w3"][0]),
        "ffn_w2": f(inputs["ffn_w2"][0]),
        "ident": hc["ident"],
        "hy_D1": hc["hy_D1"], "hy_Ere": hc["hy_Ere"], "hy_Eim": hc["hy_Eim"], "hy_G": hc["hy_G"],
        "hy_T": hc["hy_T"], "hy_zT": hc["hy_zT"], "hy_dec": hc["hy_dec"],
        "hy_w1": f(inputs["hy_ffn_w1"][0]), "hy_w2": f(inputs["hy_ffn_w2"][0]), "hy_w3": f(inputs["hy_ffn_w3"][0]),
        "hy_vecs": f(np.stack([np.asarray(inputs["hy_ffn_b1"][0]), np.asarray(inputs["hy_ffn_b2"][0]),
                               np.asarray(inputs["hy_sin_freq"][0])], axis=1)),
        "hy_bias_row": f(np.asarray(inputs["hy_bias"][0]).reshape(1, 1024)),
        "hy_biasT": f(np.asarray(inputs["hy_bias"][0]).reshape(2, 8, 64).transpose(2, 0, 1).reshape(64, 16)),
        "ropecos": hc["ropecos"],
        "ropesin": hc["ropesin"],
        "w_qk_perm": f(np.asarray(inputs["w_in"][0])[:, qk_perm_cols()]),
        "na_biasT": na_bias_table(np.asarray(inputs["na_rpb"][0], dtype=np.float32)),
        "hy_cwT": f(np.asarray(inputs["hy_conv_w"][0]).reshape(3, 12, 128).transpose(2, 1, 0)),
        "hy_cbT": f(np.asarray(inputs["hy_conv_b"][0]).reshape(12, 128).T),
        "identf": hc["identf"],
        "onesf": hc["onesf"],
    }
    maps = []
    for b in range(8):
        m = dict(shared)
        m["x"] = x[b]
        m["ctx"] = ctx[b]
        ccb = np.stack([c[b], c_ctx], axis=1)
        m["cc"] = f(ccb.reshape(8, 128, 2).transpose(1, 0, 2))
        maps.append(m)
    return maps


def kernel(**inputs):
    if "nc" not in _NC:
        _NC["nc"] = build_nc(False)
    nc = _NC["nc"]
    maps = make_in_maps(inputs)
    res = run_bass_kernel_spmd(nc, maps, core_ids=list(range(8)))
    out = np.stack([np.asarray(r["out"], dtype=np.float32) for r in res.results], axis=0)
    return out
```

```python
import math
import numpy as np
import ml_dtypes
import concourse.bass as bass
import concourse.mybir as mybir
from concourse.bass_utils import run_bass_kernel_spmd

F32 = mybir.dt.float32
BF16 = mybir.dt.bfloat16
U8 = mybir.dt.uint8
AF = mybir.ActivationFunctionType
ALU = mybir.AluOpType
NPBF = ml_dtypes.bfloat16

D = 1024
L = 4096
NT = 32
CTX = 256
EPS = 1e-6
NDMA = 8


class Tok:
    __slots__ = ("sem", "val")

    def __init__(self, sem, val):
        self.sem = sem
        self.val = val


class Buf:
    def __init__(self, name="", excl=False):
        self.name = name
        self.w = None
        self.readers = {}
        self.excl = excl
        self.w_is_read = False
        self.w_dma = False
        self.w_extra = []


class Eng:
    def __init__(self, name, sem, dsems=None):
        self.name = name
        self.sem = sem
        self.n = 0
        self.prog = []
        self.waited = {}
        self.dsems = dsems or []
        self.ndma = 0
        self.dtoks = []

    def wait(self, tok):
        if tok is None:
            return
        k = id(tok.sem)
        if self.waited.get(k, 0) < tok.val:
            self.waited[k] = tok.val
            self.prog.append(("w", tok.sem, tok.val))

    def _deps(self, R, W):
        for b in list(R) + list(W):
            for t in b.w_extra:
                self.wait(t)
        for b in R:
            if b.w is not None:
                if self.name == "pe" and b.w.sem is self.sem:
                    continue
                if b.excl and b.w_is_read and b.w.sem is self.sem:
                    continue
                self.wait(b.w)
        for b in W:
            if b.w is not None:
                if not (self.name == "pe" and b.w.sem is self.sem):
                    self.wait(b.w)
            for en, t in b.readers.items():
                if en != self.name or self.name != "pe":
                    self.wait(t)

    def _mark(self, tok, R, W):
        for b in R:
            if b.excl:
                b.w = tok
                b.w_is_read = True
                b.readers = {}
            else:
                b.readers[self.name] = tok
        for b in W:
            b.w = tok
            b.w_is_read = False
            b.w_dma = False
            b.w_extra = []
            b.readers = {}

    def op(self, fn, R=(), W=()):
        self._deps(R, W)
        self.n += 1
        tok = Tok(self.sem, self.n)
        self.prog.append(("i", fn, self.sem, 1))
        self._mark(tok, R, W)
        return tok

    def dma(self, fn, R=(), W=()):
        i = self.ndma
        self.ndma += 1
        s = self.dsems[i % len(self.dsems)]
        if i >= len(self.dsems):
            self.wait(self.dtoks[i - len(self.dsems)])
        self._deps(R, W)
        tok = Tok(s, 16 * (i // len(self.dsems) + 1))
        self.dtoks.append(tok)
        self.prog.append(("i", fn, s, 16))
        for b in R:
            b.readers[self.name + "_dma%d" % i] = tok
        for b in W:
            if b.w_dma and b.w is not None:
                b.w_extra.append(b.w)
            else:
                b.w_extra = []
            b.w = tok
            b.w_dma = True
            b.w_is_read = False
            b.readers = {}
        return tok

    def replay(self, h):
        for it in self.prog:
            if it[0] == "w":
                h.wait_ge(it[1], it[2])
            else:
                it[1](h).then_inc(it[2], it[3])


_CONST = None


def _bf(a):
    return np.ascontiguousarray(np.asarray(a, dtype=np.float32).astype(NPBF))


def host_consts():
    global _CONST
    if _CONST is not None:
        return _CONST
    c = {}
    c["ident"] = _bf(np.eye(128))
    c["identf"] = np.eye(128, dtype=np.float32)
    c["onesf"] = np.ones((128, 128), np.float32)
    nf = 16
    inv = 10000.0 ** (-np.arange(nf, dtype=np.float64) / nf)
    t = np.arange(L)
    rows, cols = t // 64, t % 64
    cos = np.zeros((64, L))
    sin = np.zeros((64, L))
    for d in range(64):
        pos = rows if d < 32 else cols
        dd = d % 32
        f = dd % 16
        ang = pos * inv[f]
        cos[d] = np.cos(ang)
        sin[d] = -np.sin(ang) if dd < 16 else np.sin(ang)
    c["ropecos"] = _bf(np.concatenate([cos, cos], 0))
    c["ropesin"] = _bf(np.concatenate([sin, sin], 0))
    N = 8192
    ns = np.arange(64)[:, None]
    ks = np.arange(64)[None, :]
    th = 2 * np.pi * ns * ks / 64.0
    d1 = np.stack([np.sin(th), np.cos(th), -np.sin(th)], axis=-1)
    c["hy_D1"] = _bf(d1.reshape(64, 192))
    nf = np.arange(128)[:, None, None]
    ksv = np.arange(64)[None, :, None]
    kf = np.arange(128)[None, None, :]
    ph = 2 * np.pi * nf * (ksv + 64 * kf) / N
    c["hy_Ere"] = _bf(np.cos(ph).reshape(128, 64 * 128))
    c["hy_Eim"] = _bf((-np.sin(ph)).reshape(128, 64 * 128))
    kfv = np.arange(128)[:, None]
    na = np.arange(128)[None, :]
    a2 = 2 * np.pi * kfv * na / 128.0
    Cf, Sf = np.cos(a2), np.sin(a2)
    c["hy_G"] = _bf(np.concatenate([Cf, Sf, -Sf, Cf], axis=1))
    ksc = np.arange(64)[:, None, None]
    nav = np.arange(128)[None, :, None]
    nbv = np.arange(32)[None, None, :]
    a3 = 2 * np.pi * ksc * (nav + 128 * nbv) / N
    wk = np.where((ksc == 0) | (ksc == 32), 1.0, np.where(ksc < 32, 2.0, 0.0))
    Tm = np.stack([wk * np.cos(a3) / N, -wk * np.sin(a3) / N], axis=1)
    Tm = Tm.reshape(64, 2 * 128 * 32)
    c["hy_T"] = _bf(np.concatenate([Tm, Tm], axis=0))
    tpos = np.concatenate([np.arange(4096), 4096 - np.arange(4096)])
    tpos[4096] = 0
    tlin = np.linspace(0.0, 1.0, L, dtype=np.float32).astype(np.float64)
    tn = tlin[np.minimum(tpos, L - 1)]
    w = (2.0 * math.pi * np.arange(L, dtype=np.float32) / L).astype(np.float64)[np.minimum(tpos, L - 1)]
    bands = np.linspace(1e-4, 15, 16, dtype=np.float32).astype(np.float64)
    z = np.concatenate([tn[:, None], np.cos(bands[None, :] * w[:, None]), np.sin(-bands[None, :] * w[:, None])], axis=1)
    c["hy_zT"] = np.ascontiguousarray(z.T.astype(np.float32))
    deltas = np.abs(np.linspace(math.log(1e-2) / 1.5, math.log(1e-2) / 0.3, 512, dtype=np.float32)).astype(np.float64)
    dec = np.exp(-tn[:, None] * deltas[None, :])
    dec = dec.reshape(64, 128, 8, 64).transpose(2, 0, 1, 3)
    c["hy_dec"] = _bf(dec.reshape(8, 64, 128 * 64))
    _CONST = c
    return c


def rope_perm():
    p = np.zeros(64, np.int64)
    for d in range(64):
        base = (d // 32) * 32
        dd = d % 32
        p[d] = base + (dd + 16) % 32
    return p


def qk_perm_cols():
    p = rope_perm()
    cols = []
    for blk in range(16):
        cols.extend((blk * 64 + p).tolist())
    return np.asarray(cols, np.int64)


def na_bias_table(rpb):
    cq = np.arange(64)
    ck = np.arange(64)
    start = np.clip(cq - 8, 0, 48)
    ok = (ck[None, :] >= start[:, None]) & (ck[None, :] < start[:, None] + 16)
    coff = np.clip(ck[None, :] - cq[:, None], -15, 15) + 15
    T = rpb[:, :, coff]
    T = np.where(ok[None, None], T, np.float32(-1e30))
    T = np.ascontiguousarray(T.transpose(2, 0, 1, 3)).astype(np.float32)
    T = T.reshape(64, 8 * 15 * 64)
    return np.ascontiguousarray(np.concatenate([T, T], axis=0))


def build_nc(dbg=False, mode="full"):
    nc = bass.Bass("TRN2", target_bir_lowering=False)

    def din(name, shape, dt=F32):
        return nc.dram_tensor(name, list(shape), dt, kind="ExternalInput").ap()

    x_d = din("x", [L, D])
    ctx_d = din("ctx", [CTX, D])
    cc_d = din("cc", [128, 8, 2])
    wada_d = din("w_ada", [D, 6 * D])
    bada_d = din("b_adaT", [128, 48])
    n1g_d = din("n1gT", [128, 8])
    n2g_d = din("n2gT", [128, 8])
    fg_d = din("fg_row", [128, D])
    win_d = din("w_in", [D, 5120])
    wnao_d = din("w_na_o", [512, D])
    whyo_d = din("w_hy_o", [512, D])
    wout_d = din("w_out", [D, D])
    w1_d = din("ffn_w1", [D, 2816])
    w3_d = din("ffn_w3", [D, 2816])
    w2_d = din("ffn_w2", [2816, D])
    ident_d = din("ident", [128, 128], BF16)
    identf_d = din("identf", [128, 128])
    onesf_d = din("onesf", [128, 128])
    wqkp_d = din("w_qk_perm", [D, D])
    rcos_d = din("ropecos", [128, L], BF16)
    rsin_d = din("ropesin", [128, L], BF16)
    bias_d = din("na_biasT", [128, 8 * 15 * 64])
    cw_d = din("hy_cwT", [128, 12, 3])
    cb_d = din("hy_cbT", [128, 12])
    hyD1_d = din("hy_D1", [64, 192], BF16)
    hyEre_d = din("hy_Ere", [128, 64 * 128], BF16)
    hyEim_d = din("hy_Eim", [128, 64 * 128], BF16)
    hyG_d = din("hy_G", [128, 512], BF16)
    hyT_d = din("hy_T", [128, 2 * 128 * 32], BF16)
    hyz_d = din("hy_zT", [33, 8192])
    hydec_d = din("hy_dec", [8, 64, 128 * 64], BF16)
    hyw1_d = din("hy_w1", [33, 64])
    hyw2_d = din("hy_w2", [64, 64])
    hyw3_d = din("hy_w3", [64, 2048])
    hyv_d = din("hy_vecs", [64, 3])
    hyb_d = din("hy_biasT", [64, 16])
    hybr_d = din("hy_bias_row", [1, 1024])
    uT_d = nc.dram_tensor("uTs", [1536, L], BF16, kind="Internal").ap()
    yna_d = nc.dram_tensor("ynas", [512, L], BF16, kind="Internal").ap()
    yhy_d = nc.dram_tensor("yhys", [512, L], BF16, kind="Internal").ap()
    out_d = nc.dram_tensor("out", [L, D], F32, kind="ExternalOutput").ap()
    x1_d = nc.dram_tensor("x1s", [L, D], F32, kind="Internal").ap()
    dbg_d = None
    if dbg:
        dbg_d = nc.dram_tensor("dbg", [128, 4096], F32, kind="ExternalOutput").ap()
        dbg2_d = nc.dram_tensor("dbg2", [128, 4096], BF16, kind="ExternalOutput").ap()
        dbg3_d = nc.dram_tensor("dbg3", [64, 4096], BF16, kind="ExternalOutput").ap()
        dbg4_d = nc.dram_tensor("dbg4", [64, 8192], BF16, kind="ExternalOutput").ap()
        dbg5_d = nc.dram_tensor("dbg5", [128, 8192], BF16, kind="ExternalOutput").ap()

    SB = (nc.sbuf_bytes_remaining - 1024) // 64 * 64
    arena = nc.alloc_sbuf_tensor("arena", [128, SB], U8)
    banks = [nc.alloc_psum_tensor("pb%d" % i, [128, 512], F32) for i in range(8)]
    PB = [Buf("pb%d" % i, excl=True) for i in range(8)]

    class Arena:
        def __init__(self):
            self.off = 0
            self.marks = []
            self.on_pop = None

        def alloc(self, cols, dt, shape=None, parts=128):
            esz = 4 if dt == F32 else 2
            nb = (cols * esz + 63) // 64 * 64
            assert self.off + nb <= SB, ("SBUF overflow", self.off, nb, SB)
            a = arena[0:parts, self.off:self.off + cols * esz].bitcast(dt)
            self.off += nb
            return a

        def push(self):
            self.marks.append(self.off)

        def pop(self):
            self.off = self.marks.pop()
            if self.on_pop is not None:
                self.on_pop()

    ar = Arena()

    sems = {}
    import contextlib
    stack = contextlib.ExitStack()
    with stack:
        def sem(name):
            return stack.enter_context(nc.semaphore(name))

        pe = Eng("pe", sem("s_pe"))
        act = Eng("act", sem("s_act"), [sem("s_actd%d" % i) for i in range(NDMA)])
        dve = Eng("dve", sem("s_dve"))
        pool = Eng("pool", sem("s_pool"), [sem("s_poold%d" % i) for i in range(NDMA)])
        sp = Eng("sp", sem("s_sp"), [sem("s_spd%d" % i) for i in range(NDMA)])

        engs = [pe, act, dve, pool, sp]

        def barrier():
            toks = [Tok(e.sem, e.n) for e in engs if e.n > 0]
            for e in engs:
                toks += e.dtoks[-len(e.dsems):] if e.dsems else []
            for e in engs:
                for t in toks:
                    e.wait(t)

        ar.on_pop = barrier

        def mm(out, lhsT, rhs, start, stop, R, W):
            return pe.op(lambda h: h.matmul(out, lhsT=lhsT, rhs=rhs, start=start, stop=stop), R, W)

        def tr(out, in_, ident, R, W):
            return pe.op(lambda h: h.transpose(out=out, in_=in_, identity=ident), R, W)

        def actf(out, in_, func, R, W, bias=None, scale=None, accum_out=None):
            kw = {}
            if bias is not None:
                kw["bias"] = bias
            if scale is not None:
                kw["scale"] = scale
            if accum_out is not None:
                kw["accum_out"] = accum_out
            return act.op(lambda h: h.activation(out=out, in_=in_, func=func, **kw), R, W)

        def tt(eng, out, in0, in1, op, R, W):
            return eng.op(lambda h: h.tensor_tensor(out=out, in0=in0, in1=in1, op=op), R, W)

        def ts(eng, out, in0, s1, s2, op0, op1, R, W):
            if op1 is None:
                return eng.op(lambda h: h.tensor_scalar(out=out, in0=in0, scalar1=s1, scalar2=None, op0=op0), R, W)
            return eng.op(lambda h: h.tensor_scalar(out=out, in0=in0, scalar1=s1, scalar2=s2, op0=op0, op1=op1), R, W)

        def stt(eng, out, in0, s, in1, op0, op1, R, W):
            return eng.op(lambda h: h.scalar_tensor_tensor(out=out, in0=in0, scalar=s, in1=in1, op0=op0, op1=op1), R, W)

        def cp(eng, out, in_, R, W):
            return eng.op(lambda h: h.tensor_copy(out=out, in_=in_), R, W)

        def dma(eng, out, in_, R, W):
            return eng.dma(lambda h: h.dma_start(out=out, in_=in_), R, W)

        def memset(eng, ap, v, W):
            return eng.op(lambda h: h.memset(ap, v), (), W)

        ident = ar.alloc(128, BF16)
        identf = ar.alloc(128, F32)
        onesf = ar.alloc(128, F32)
        B_const = Buf("const")
        dma(sp, ident, ident_d, (), (B_const,))
        dma(sp, identf, identf_d, (), (B_const,))
        dma(sp, onesf, onesf_d, (), (B_const,))
        cc = ar.alloc(16, F32).rearrange("p (k j) -> p k j", j=2)
        dma(sp, cc, cc_d, (), (B_const,))
        badaT = ar.alloc(48, F32)
        dma(sp, badaT, bada_d, (), (B_const,))
        n1g = ar.alloc(8, F32)
        n2g = ar.alloc(8, F32)
        dma(sp, n1g, n1g_d, (), (B_const,))
        dma(sp, n2g, n2g_d, (), (B_const,))
        fgrow = ar.alloc(D, F32)
        dma(sp, fgrow, fg_d, (), (B_const,))

        B_mod = Buf("mod")
        B_mod2 = Buf("mod2")
        B_scc = Buf("scc")
        scc = ar.alloc(16, BF16).rearrange("p (k j) -> p k j", j=2)
        actf(scc, cc, AF.Silu, (B_const,), (B_scc,))
        modT = ar.alloc(96, F32).rearrange("p (c j) -> p c j", j=2)
        wada_v = wada_d.rearrange("(k p) n -> p k n", p=128)

        def p0_load(pc, wa_, B_wa_):
            dma(pool, wa_, wada_v[:, :, pc * 512:(pc + 1) * 512], (), (B_wa_,))

        def p0_compute(pc, wa_, B_wa_, Bm, pb):
            for cj in range(4):
                for kc in range(8):
                    mm(banks[pb][:, cj * 2:cj * 2 + 2], wa_[:, kc, cj * 128:(cj + 1) * 128], scc[:, kc, :],
                       kc == 0, kc == 7, (B_wa_, B_scc), (PB[pb],))
            for j in range(2):
                tt(dve, modT[:, pc * 4:(pc + 1) * 4, j], banks[pb][:, j:8:2], badaT[:, pc * 4:(pc + 1) * 4],
                   ALU.add, (PB[pb], B_const), (Bm,))

        ar.push()
        wa = [ar.alloc(8 * 512, BF16).rearrange("p (k n) -> p k n", k=8) for _ in range(2)]
        B_wa = [Buf("wa0"), Buf("wa1")]
        for pc in range(4):
            p0_load(pc, wa[pc % 2], B_wa[pc % 2])
            p0_compute(pc, wa[pc % 2], B_wa[pc % 2], B_mod, pc % 2)
        ar.pop()
        A1 = ar.alloc(8, F32)
        A1c = ar.alloc(8, F32)
        A2 = ar.alloc(8, F32)
        stt(dve, A1, modT[:, 8:16, 0], 1.0, n1g, ALU.add, ALU.mult, (B_mod, B_const), (B_mod,))
        stt(dve, A1c, modT[:, 8:16, 1], 1.0, n1g, ALU.add, ALU.mult, (B_mod, B_const), (B_mod,))
        B1 = modT[:, 0:8, 0]
        B1c = modT[:, 0:8, 1]
        B2 = modT[:, 24:32, 0]

        def bcast_rows(vecT, dst):
            ar.push()
            dg = ar.alloc(8 * 128, F32).rearrange("p (k n) -> p k n", k=8)
            B_dg = Buf("dg")
            for kc in range(8):
                ts(dve, dg[:, kc, :], identf, vecT[:, kc:kc + 1], None, ALU.mult, None, (B_mod, B_mod2, B_const), (B_dg,))
            for hh in range(2):
                mm(banks[2 + hh][:, :], onesf, dg[:, hh * 4:(hh + 1) * 4, :].rearrange("p k n -> p (k n)"),
                   True, True, (B_dg, B_const), (PB[2 + hh],))
                actf(dst[:, hh * 512:(hh + 1) * 512], banks[2 + hh][:, :], AF.Copy, (PB[2 + hh],), (B_mod,))
            ar.pop()

        g1row = ar.alloc(D, F32)
        g2row = ar.alloc(D, F32)

        def p0_finish():
            stt(dve, A2, modT[:, 32:40, 0], 1.0, n2g, ALU.add, ALU.mult, (B_mod2, B_const), (B_mod,))
            bcast_rows(modT[:, 16:24, 0], g1row)
            bcast_rows(modT[:, 40:48, 0], g2row)

        def rstd_from_ssq(stat, B_stat):
            ts(dve, stat[:, 1:2], stat[:, 0:1], 1.0 / D, EPS, ALU.mult, ALU.add, (B_stat,), (B_stat,))
            actf(stat[:, 3:4], stat[:, 1:2], AF.Sqrt, (B_stat,), (B_stat,))
            dve.op(lambda h: h.reciprocal(out=stat[:, 2:3], in_=stat[:, 3:4]), (B_stat,), (B_stat,))

        def norm_tile(xt, B_xt, xn, B_xn, junk, B_junk, stat, B_stat):
            actf(junk, xt, AF.Square, (B_xt,), (B_junk, B_stat), accum_out=stat[:, 0:1])
            rstd_from_ssq(stat, B_stat)
            ts(dve, xn, xt, stat[:, 2:3], None, ALU.mult, None, (B_xt, B_stat), (B_xn,))

        def transpose_mod(xn, B_xn, pbi, dstT, tok0, B_dst, Asc, Bsc, pbi2=None):
            pbv = banks[pbi][:, :].bitcast(BF16)
            if pbi2 is None:
                for kc in range(8):
                    tr(pbv[:, kc * 128:(kc + 1) * 128], xn[:, kc * 128:(kc + 1) * 128], ident, (B_xn, B_const), (PB[pbi],))
                for kc in range(8):
                    if kc % 2 == 0:
                        actf(dstT[:, kc, tok0:tok0 + 128], pbv[:, kc * 128:(kc + 1) * 128], AF.Identity,
                             (PB[pbi], B_mod), (B_dst,), bias=Bsc[:, kc:kc + 1], scale=Asc[:, kc:kc + 1])
                    else:
                        ts(dve, dstT[:, kc, tok0:tok0 + 128], pbv[:, kc * 128:(kc + 1) * 128], Asc[:, kc:kc + 1],
                           Bsc[:, kc:kc + 1], ALU.mult, ALU.add, (PB[pbi], B_mod), (B_dst,))
                return
            pbv2 = banks[pbi2][:, :].bitcast(BF16)
            for kc in range(8):
                if kc < 5:
                    tr(pbv[:, kc * 128:(kc + 1) * 128], xn[:, kc * 128:(kc + 1) * 128], ident, (B_xn, B_const), (PB[pbi],))
                else:
                    tr(pbv2[:, kc * 128:(kc + 1) * 128], xn[:, kc * 128:(kc + 1) * 128], ident, (B_xn, B_const), (PB[pbi2],))
            for kc in range(5):
                actf(dstT[:, kc, tok0:tok0 + 128], pbv[:, kc * 128:(kc + 1) * 128], AF.Identity,
                     (PB[pbi], B_mod), (B_dst,), bias=Bsc[:, kc:kc + 1], scale=Asc[:, kc:kc + 1])
            for kc in range(5, 8):
                ts(dve, dstT[:, kc, tok0:tok0 + 128], pbv2[:, kc * 128:(kc + 1) * 128], Asc[:, kc:kc + 1],
                   Bsc[:, kc:kc + 1], ALU.mult, ALU.add, (PB[pbi2], B_mod), (B_dst,))

        B_yna = Buf("yna")
        B_ynad = Buf("ynad")
        B_yhyd = [Buf("yhyd%d" % i) for i in range(8)]
        yna_v = yna_d.rearrange("(c p) t -> p c t", p=128)
        yhy_v = yhy_d.rearrange("(c p) t -> p c t", p=128)
        x_v = x_d.rearrange("(n p) d -> n p d", p=128)
        winv = win_d.rearrange("(k p) n -> p k n", p=128)

        ar.push()
        hT = ar.alloc(8 * L, BF16).rearrange("p (k t) -> p k t", k=8)
        ynaP = ar.alloc(L, BF16)
        B_hT = [Buf("hT%d" % i) for i in range(NT)]
        hcT = ar.alloc(8 * CTX, BF16).rearrange("p (k t) -> p k t", k=8)
        B_hcT = Buf("hcT")
        ar.push()
        NB1 = 6
        xt2 = [ar.alloc(D, F32) for _ in range(NB1)]
        B_xt2 = [Buf("xt2_%d" % i) for i in range(NB1)]
        xn2 = [ar.alloc(D, BF16) for _ in range(NB1)]
        B_xn2 = [Buf("xn2_%d" % i) for i in range(NB1)]
        junk = ar.alloc(D, BF16)
        B_junk = Buf("junk1")
        stat = [ar.alloc(4, F32) for _ in range(NB1)]
        B_stat = [Buf("st1_%d" % i) for i in range(NB1)]
        ctx_v = ctx_d.rearrange("(n p) d -> n p d", p=128)
        NTI = NT + 2
        wa2 = [ar.alloc(8 * 512, BF16).rearrange("p (k n) -> p k n", k=8) for _ in range(2)]
        B_wa2 = [Buf("wa2_0"), Buf("wa2_1")]
        p0_load(4, wa2[0], B_wa2[0])
        p0_load(5, wa2[1], B_wa2[1])
        for k in range(NTI + 3):
            k0, k1, k2, k3 = k, k - 1, k - 2, k - 3
            if 0 <= k0 < NTI:
                s = k0 % NB1
                src = x_v[k0] if k0 < NT else ctx_v[k0 - NT]
                dma(sp, xt2[s], src, (), (B_xt2[s],))
                actf(junk, xt2[s], AF.Square, (B_xt2[s],), (B_junk, B_stat[s]), accum_out=stat[s][:, 0:1])
            if 0 <= k1 < NTI:
                s = k1 % NB1
                ts(dve, stat[s][:, 1:2], stat[s][:, 0:1], 1.0 / D, EPS, ALU.mult, ALU.add, (B_stat[s],), (B_stat[s],))
                actf(stat[s][:, 3:4], stat[s][:, 1:2], AF.Sqrt, (B_stat[s],), (B_stat[s],))
            if 0 <= k2 < NTI:
                s = k2 % NB1
                dve.op(lambda h, o_=stat[s][:, 2:3], i_=stat[s][:, 3:4]: h.reciprocal(out=o_, in_=i_), (B_stat[s],), (B_stat[s],))
                ts(dve, xn2[s], xt2[s], stat[s][:, 2:3], None, ALU.mult, None, (B_xt2[s], B_stat[s]), (B_xn2[s],))
            if k % 4 == 3 and k // 4 < 8:
                m_ = k // 4
                p0_compute(4 + m_, wa2[m_ % 2], B_wa2[m_ % 2], B_mod2, 6 + m_ % 2)
                if 4 + m_ + 2 < 12:
                    p0_load(4 + m_ + 2, wa2[m_ % 2], B_wa2[m_ % 2])
            if 0 <= k3 < NTI:
                s = k3 % NB1
                pA, pB = 2 * (k3 % 4 // 1 % 4 % 4) % 8, 0
                pA = 2 * (k3 % 4)
                pB = pA + 1
                if k3 < NT:
                    transpose_mod(xn2[s], B_xn2[s], pA, hT, k3 * 128, B_hT[k3], A1, B1, pbi2=pB)
                else:
                    transpose_mod(xn2[s], B_xn2[s], pA, hcT, (k3 - NT) * 128, B_hcT, A1c, B1c, pbi2=pB)
        p0_finish()
        ar.pop()

        B_uT = [Buf("uT%d" % j) for j in range(12)]
        if mode in ("full", "hy_test"):
            ar.push()
            cw = ar.alloc(36, F32).rearrange("p (j k) -> p j k", k=3)
            cb = ar.alloc(12, F32)
            B_cw = Buf("cw")
            dma(sp, cw, cw_d, (), (B_cw,))
            dma(sp, cb, cb_d, (), (B_cw,))
            Wh = [ar.alloc(8 * 128, BF16).rearrange("p (k n) -> p k n", k=8) for _ in range(2)]
            B_Wh = [Buf("Wh0"), Buf("Wh1")]
            raws = [ar.alloc(L + 2, F32) for _ in range(2)]
            B_raws = [Buf("raw0"), Buf("raw1")]
            acc = ar.alloc(L, F32)
            B_acc = Buf("acc")
            outb = [ar.alloc(L, BF16) for _ in range(2)]
            B_outb = [Buf("outb0"), Buf("outb1")]
            for rr_ in range(2):
                memset(pool, raws[rr_][:, 0:1], 0.0, (B_raws[rr_],))
                memset(pool, raws[rr_][:, L + 1:L + 2], 0.0, (B_raws[rr_],))
            for j in range(12):
                s = j % 2
                raw = raws[s]
                B_raw = B_raws[s]
                dma(pool, Wh[s], winv[:, :, 1536 + j * 128:1536 + (j + 1) * 128], (), (B_Wh[s],))
                for tb in range(8):
                    pb = tb % 2
                    for kc in range(8):
                        mm(banks[pb][:, :], Wh[s][:, kc, :], hT[:, kc, tb * 512:(tb + 1) * 512], kc == 0, kc == 7,
                           [B_Wh[s]] + B_hT[tb * 4:tb * 4 + 4], (PB[pb],))
                    actf(raw[:, 1 + tb * 512:1 + (tb + 1) * 512], banks[pb][:, :], AF.Copy, (PB[pb],), (B_raw,))
                ts(dve, acc, raw[:, 1:L + 1], cw[:, j, 1:2], cb[:, j:j + 1], ALU.mult, ALU.add, (B_raw, B_cw), (B_acc,))
                stt(dve, acc, raw[:, 0:L], cw[:, j, 0:1], acc, ALU.mult, ALU.add, (B_raw, B_cw, B_acc), (B_acc,))
                stt(dve, outb[s], raw[:, 2:L + 2], cw[:, j, 2:3], acc, ALU.mult, ALU.add, (B_raw, B_cw, B_acc), (B_outb[s],))
                dma(sp, uT_d[j * 128:(j + 1) * 128, :], outb[s], (B_outb[s],), (B_uT[j],))
            ar.pop()

        if mode in ("full", "na_test"):
            rcos = ar.alloc(L, BF16)
            rsin = ar.alloc(L, BF16)
            biasT2 = ar.alloc(8 * 15 * 64, BF16)
            biasT = biasT2.rearrange("p (h r c) -> p h r c", h=8, r=15)
            B_tab = Buf("natab")
            dma(sp, rcos, rcos_d, (), (B_tab,))
            dma(sp, rsin, rsin_d, (), (B_tab,))
            dma(pool, biasT2, bias_d, (), (B_tab,))
            c8 = ar.alloc(1, F32)
            B_c8 = Buf("c8")
            memset(pool, c8, 0.125, (B_c8,))
            qrT = ar.alloc(L, BF16)
            qpT = ar.alloc(L, BF16)
            krT = ar.alloc(L, BF16)
            B_q = [Buf("q%d" % i) for i in range(8)]
            Vaug = ar.alloc(NT * 2 * 65, BF16).rearrange("p (t h d) -> p t h d", t=NT, h=2)
            B_V = [Buf("V%d" % i) for i in range(NT)]
            Vc = ar.alloc(2 * 2 * 65, BF16).rearrange("p (t h d) -> p t h d", t=2, h=2)
            B_Vc = Buf("Vc")
            kcT = ar.alloc(CTX, BF16)
            B_kc = Buf("kcT")
            wqkp_v = wqkp_d.rearrange("(k p) n -> p k n", p=128)
            Wp_all = [[ar.alloc(8 * 128, BF16).rearrange("p (k n) -> p k n", k=8) for _ in range(5)] for _ in range(2)]
            B_Wp_all = [Buf("Wp0"), Buf("Wp1")]

            def load_Wp(pr_):
                c0_ = pr_ * 128
                W_, B_ = Wp_all[pr_ % 2], B_Wp_all[pr_ % 2]
                dma(pool, W_[0], winv[:, :, c0_:c0_ + 128], (), (B_,))
                dma(pool, W_[1], wqkp_v[:, :, c0_:c0_ + 128], (), (B_,))
                dma(pool, W_[2], winv[:, :, 512 + c0_:512 + c0_ + 128], (), (B_,))
                dma(pool, W_[3], wqkp_v[:, :, 512 + c0_:512 + c0_ + 128], (), (B_,))
                dma(pool, W_[4], winv[:, :, 1024 + c0_:1024 + c0_ + 128], (), (B_,))
            t1 = [ar.alloc(512, F32) for _ in range(2)]
            t2 = [ar.alloc(512, F32) for _ in range(2)]
            B_t1 = [Buf("t1_0"), Buf("t1_1")]
            B_t2 = [Buf("t2_0"), Buf("t2_1")]
            PT = [ar.alloc(7 * 64, BF16) for _ in range(4)]
            B_PT = [Buf("PT%d" % i) for i in range(4)]
            rec = [ar.alloc(2, F32) for _ in range(2)]
            B_rec = [Buf("rec0"), Buf("rec1")]
            ynorm = [ar.alloc(128, BF16) for _ in range(2)]
            B_yn = [Buf("yn0"), Buf("yn1")]
            memset(pool, Vaug[:, :, :, 64:65], 1.0, B_V)
            memset(pool, Vc[:, :, :, 64:65], 1.0, (B_Vc,))
            sidx = 0
            import os as _os
            _NP = int(_os.environ.get('NA_PAIRS', '4'))
            _NTT = int(_os.environ.get('NA_NT', str(NT)))
            _STG = int(_os.environ.get('NA_STAGE', '9'))
            for pr in range(_NP):
                c0 = pr * 128
                Wp, B_Wp = Wp_all[pr % 2], B_Wp_all[pr % 2]
                if pr == 0:
                    load_Wp(0)
                if pr + 1 < _NP:
                    load_Wp(pr + 1)
                for kc in range(8 if _STG >= 1 else 0):
                    mm(banks[6][:, 0:CTX], Wp[2][:, kc, :], hcT[:, kc, :], kc == 0, kc == 7, (B_Wp, B_hcT), (PB[6],))
                if _STG >= 1:
                    actf(kcT, banks[6][:, 0:CTX], AF.Copy, (PB[6],), (B_kc,))
                for ct in range(2 if _STG >= 2 else 0):
                    for kc in range(8):
                        mm(banks[7][:, ct * 128:(ct + 1) * 128], hcT[:, kc, ct * 128:(ct + 1) * 128], Wp[4][:, kc, :],
                           kc == 0, kc == 7, (B_Wp, B_hcT), (PB[7],))
                if _STG >= 2:
                    cp(dve, Vc[:, :, :, 0:64], banks[7][:, 0:256].rearrange("p (t h d) -> p t h d", t=2, h=2),
                       (PB[7],), (B_Vc,))
                for tb in range(8 if _STG >= 3 else 0):
                    tsl = slice(tb * 512, (tb + 1) * 512)
                    RH = [B_Wp] + B_hT[tb * 4:tb * 4 + 4]
                    for wi in range(4):
                        for kc in range(8):
                            mm(banks[wi][:, :], Wp[wi][:, kc, :], hT[:, kc, tsl], kc == 0, kc == 7, RH, (PB[wi],))
                    s = tb % 2
                    _SUB = int(_os.environ.get('NA_SUB', '9'))
                    if _SUB >= 1:
                        tt(dve, t1[s], banks[0][:, :], rcos[:, tsl], ALU.mult, (PB[0], B_tab), (B_t1[s],))
                    if _SUB >= 2:
                        actf(qpT[:, tsl], banks[0][:, :], AF.Identity, (PB[0], B_c8), (B_q[tb],), scale=c8[:, 0:1])
                    if _SUB >= 3:
                        stt(dve, t2[s], banks[1][:, :], 0.125, rsin[:, tsl], ALU.mult, ALU.mult, (PB[1], B_tab), (B_t2[s],))
                    if _SUB >= 4:
                        stt(dve, qrT[:, tsl], t1[s], 0.125, t2[s], ALU.mult, ALU.add, (B_t1[s], B_t2[s]), (B_q[tb],))
                    if _SUB >= 5:
                        tt(dve, t1[s], banks[2][:, :], rcos[:, tsl], ALU.mult, (PB[2], B_tab), (B_t1[s],))
                        tt(dve, t2[s], banks[3][:, :], rsin[:, tsl], ALU.mult, (PB[3], B_tab), (B_t2[s],))
                    if _SUB >= 6:
                        tt(pool, krT[:, tsl], t1[s], t2[s], ALU.add, (B_t1[s], B_t2[s]), (B_q[tb],))
                for ti in range(NT if _STG >= 4 else 0):
                    pb = 4 + (ti % 2)
                    for kc in range(8):
                        mm(banks[pb][:, 0:128], hT[:, kc, ti * 128:(ti + 1) * 128], Wp[4][:, kc, :], kc == 0, kc == 7,
                           (B_Wp, B_hT[ti]), (PB[pb],))
                    actf(Vaug[:, ti, :, 0:64], banks[pb][:, 0:128].rearrange("p (h d) -> p h d", h=2), AF.Copy,
                         (PB[pb],), (B_V[ti],))
                def qk_list(ti, rr, hl, sb):
                    r = 2 * ti + rr
                    row0 = min(max(r - 4, 0), 56)
                    if row0 % 2 == 0:
                        tiles = [("p", row0 + 2 * j) for j in range(4)]
                    else:
                        tiles = [("o", row0)] + [("p", row0 + 1 + 2 * j) for j in range(3)] + [("e", row0 + 7)]
                    qs = slice(r * 64, (r + 1) * 64)
                    h = pr * 2 + hl
                    hp = slice(hl * 64, (hl + 1) * 64)
                    S = banks[sb]
                    RK = [B_q[tb] for tb in sorted(set([(row0 * 64) // 512, (row0 * 64 + 511) // 512, (r * 64) // 512]))]
                    ops = []
                    for ct in range(2):
                        ops.append((S[:, ct * 64:(ct + 1) * 64], kcT[hp, ct * 128:(ct + 1) * 128], qpT[hp, qs], True, True,
                                    [B_kc] + RK, (PB[sb],)))
                    for si, (kind, ka) in enumerate(tiles):
                        sl_ = slice((2 + si) * 64, (3 + si) * 64)
                        roff = ka - r + 7
                        if kind == "p":
                            o_ap = S[:, sl_]
                            k_ap = krT[hp, ka * 64:ka * 64 + 128]
                            b_ap = biasT[hp, h, roff:roff + 2, :].rearrange("p r c -> p (r c)")
                        elif kind == "o":
                            o_ap = S[64:128, sl_]
                            k_ap = krT[hp, ka * 64:ka * 64 + 64]
                            b_ap = biasT[hp, h, roff, :]
                        else:
                            o_ap = S[0:64, sl_]
                            k_ap = krT[hp, ka * 64:ka * 64 + 64]
                            b_ap = biasT[hp, h, roff, :]
                        ops.append((o_ap, k_ap, qrT[hp, qs], True, False, RK, (PB[sb],)))
                        ops.append((o_ap, b_ap, ident[hp, hp], False, True, (B_tab, B_const), (PB[sb],)))
                    return ops, tiles

                def do_exp(sb, tiles):
                    ns = 2 + len(tiles)
                    actf(PT[sb][:, 0:ns * 64], banks[sb][:, 0:ns * 64], AF.Exp, (PB[sb],), (B_PT[sb],))

                def pv(ti, rr, hl, sb, tiles):
                    ob = 4 + (ti % 2)
                    O = banks[ob][rr * 64:(rr + 1) * 64, hl * 65:(hl + 1) * 65]
                    for ct in range(2):
                        mm(O, PT[sb][:, ct * 64:(ct + 1) * 64], Vc[:, ct, hl, :], ct == 0, False,
                           (B_PT[sb], B_Vc), (PB[ob],))
                    for si, (kind, ka) in enumerate(tiles):
                        sl_ = slice((2 + si) * 64, (3 + si) * 64)
                        last = si == len(tiles) - 1
                        if kind == "p":
                            mm(O, PT[sb][:, sl_], Vaug[:, ka // 2, hl, :], False, last, (B_PT[sb], B_V[ka // 2]), (PB[ob],))
                        elif kind == "o":
                            mm(O, PT[sb][64:128, sl_], Vaug[64:128, ka // 2, hl, :], False, last,
                               (B_PT[sb], B_V[ka // 2]), (PB[ob],))
                        else:
                            mm(O, PT[sb][0:64, sl_], Vaug[0:64, ka // 2, hl, :], False, last,
                               (B_PT[sb], B_V[ka // 2]), (PB[ob],))

                def finish(ti):
                    ob = 4 + (ti % 2)
                    s = ti % 2
                    Ov = banks[ob][:, 0:130].rearrange("p (h d) -> p h d", h=2)
                    dve.op(lambda h_, o=rec[s], i=Ov[:, :, 64]: h_.reciprocal(out=o, in_=i), (PB[ob],), (B_rec[s],))
                    for hl in range(2):
                        ts(dve, ynorm[s][:, hl * 64:(hl + 1) * 64], Ov[:, hl, 0:64], rec[s][:, hl:hl + 1], None, ALU.mult, None,
                           (PB[ob], B_rec[s]), (B_yn[s],))
                    tb_ = 6 + s
                    pbv = banks[tb_][:, :].bitcast(BF16)
                    tr(pbv[:, 0:128], ynorm[s], ident, (B_yn[s], B_const), (PB[tb_],))
                    actf(ynaP[:, ti * 128:(ti + 1) * 128], pbv[:, 0:128], AF.Copy, (PB[tb_],), (B_yna,))

                rows_ = [(ti, rr) for ti in range(_NTT) for rr in range(2)]
                pend = None
                for idx, (ti, rr) in enumerate(rows_):
                    sb0 = (idx % 2) * 2
                    ops0, tiles = qk_list(ti, rr, 0, sb0)
                    ops1, _ = qk_list(ti, rr, 1, sb0 + 1)
                    for oa, ob_ in zip(ops0, ops1):
                        mm(*oa)
                        mm(*ob_)
                    do_exp(sb0, tiles)
                    do_exp(sb0 + 1, tiles)
                    if pend is not None:
                        pti, prr, psb, ptiles = pend
                        pv(pti, prr, 0, psb, ptiles)
                        pv(pti, prr, 1, psb + 1, ptiles)
                        if prr == 1:
                            finish(pti)
                    pend = (ti, rr, sb0, tiles)
                if pend is not None:
                    pti, prr, psb, ptiles = pend
                    pv(pti, prr, 0, psb, ptiles)
                    pv(pti, prr, 1, psb + 1, ptiles)
                    finish(pti)
                dma(sp, yna_v[:, pr, :], ynaP, (B_yna,), (B_ynad,))
                if dbg and pr == 0:
                    dma(sp, dbg2_d, ynaP, (B_yna,), (B_ynad,))
        else:
            memset(pool, ynaP, 0.0, (B_yna,))
            for c4 in range(4):
                dma(sp, yna_v[:, c4, :], ynaP, (B_yna,), (B_ynad,))
        if dbg:
            pass
        ar.pop()

        if mode in ("full", "hy_test"):
            ar.push()
            Ere2 = ar.alloc(64 * 128, BF16)
            Eim2 = ar.alloc(64 * 128, BF16)
            Ere = Ere2.rearrange("p (k f) -> p k f", k=64)
            Eim = Eim2.rearrange("p (k f) -> p k f", k=64)
            T2 = ar.alloc(2 * 128 * 32, BF16)
            Tt = T2.rearrange("p (m a b) -> p m a b", m=2, a=128)
            Gt = ar.alloc(512, BF16)
            D1 = ar.alloc(192, BF16)
            hb = ar.alloc(16, F32)
            w3b = ar.alloc(2048, BF16)
            hidT = ar.alloc(8192, BF16)
            B_hyc = Buf("hyconst")
            dma(sp, Ere2, hyEre_d, (), (B_hyc,))
            dma(sp, Eim2, hyEim_d, (), (B_hyc,))
            dma(sp, T2, hyT_d, (), (B_hyc,))
            dma(sp, Gt, hyG_d, (), (B_hyc,))
            dma(sp, D1[0:64, :], hyD1_d, (), (B_hyc,))
            dma(sp, hb[0:64, :], hyb_d, (), (B_hyc,))
            hbrow = ar.alloc(1024, F32)
            dma(sp, hbrow[0:1, :], hybr_d, (), (B_hyc,))
            dma(pool, w3b[0:64, :], hyw3_d, (), (B_hyc,))
            ar.push()
            zT = ar.alloc(8192, F32)
            w1s = ar.alloc(64, F32)
            w2s = ar.alloc(64, F32)
            vecs = ar.alloc(8, F32)
            B_mlp = Buf("mlpc")
            dma(sp, zT[0:33, :], hyz_d, (), (B_mlp,))
            dma(sp, w1s[0:33, :], hyw1_d, (), (B_mlp,))
            dma(sp, w2s[0:64, :], hyw2_d, (), (B_mlp,))
            dma(sp, vecs[0:64, 0:3], hyv_d, (), (B_mlp,))
            ts(dve, vecs[0:64, 3:4], vecs[0:64, 2:3], 1.0 / 3.0, None, ALU.mult, None, (B_mlp,), (B_mlp,))
            tt(dve, vecs[0:64, 4:5], vecs[0:64, 3:4], vecs[0:64, 0:1], ALU.mult, (B_mlp,), (B_mlp,))
            tt(dve, vecs[0:64, 5:6], vecs[0:64, 3:4], vecs[0:64, 1:2], ALU.mult, (B_mlp,), (B_mlp,))
            s1 = [ar.alloc(512, F32) for _ in range(4)]
            sq = [ar.alloc(512, F32) for _ in range(4)]
            h1 = [ar.alloc(512, F32) for _ in range(4)]
            B_s1 = [Buf("s1_%d" % i) for i in range(4)]
            B_sq = [Buf("sq_%d" % i) for i in range(4)]
            B_h1 = [Buf("h1_%d" % i) for i in range(4)]
            B_hid = Buf("hid")
            for blk in range(16):
                s_ = blk % 4
                bsl = slice(blk * 512, (blk + 1) * 512)
                pa, pb2 = 2 * s_, 2 * s_ + 1
                mm(banks[pa][0:64, :], w1s[0:33, :], zT[0:33, bsl], True, True, (B_mlp,), (PB[pa],))
                actf(s1[s_][0:64, :], banks[pa][0:64, :], AF.Sin, (PB[pa], B_mlp), (B_s1[s_],),
                     bias=vecs[0:64, 4:5], scale=vecs[0:64, 3:4])
                tt(dve, sq[s_][0:64, :], s1[s_][0:64, :], s1[s_][0:64, :], ALU.mult, (B_s1[s_],), (B_sq[s_],))
                ts(dve, sq[s_][0:64, :], sq[s_][0:64, :], -4.0, 3.0, ALU.mult, ALU.add, (B_sq[s_],), (B_sq[s_],))
                tt(dve, h1[s_][0:64, :], sq[s_][0:64, :], s1[s_][0:64, :], ALU.mult, (B_sq[s_], B_s1[s_]), (B_h1[s_],))
                mm(banks[pb2][0:64, :], w2s[0:64, :], h1[s_][0:64, :], True, True, (B_mlp, B_h1[s_]), (PB[pb2],))
                actf(s1[s_][0:64, :], banks[pb2][0:64, :], AF.Sin, (PB[pb2], B_mlp), (B_s1[s_],),
                     bias=vecs[0:64, 5:6], scale=vecs[0:64, 3:4])
                tt(dve, sq[s_][0:64, :], s1[s_][0:64, :], s1[s_][0:64, :], ALU.mult, (B_s1[s_],), (B_sq[s_],))
                ts(dve, sq[s_][0:64, :], sq[s_][0:64, :], -4.0, 3.0, ALU.mult, ALU.add, (B_sq[s_],), (B_sq[s_],))
                tt(dve, hidT[0:64, bsl], sq[s_][0:64, :], s1[s_][0:64, :], ALU.mult, (B_sq[s_], B_s1[s_]), (B_hid,))
            memset(pool, hidT[0:64, 4096:4097], 0.0, (B_hid,))
            ar.pop()
            zt = ar.alloc(L, BF16)
            zp = ar.alloc(L, BF16)
            B_zp = Buf("zp")
            x1t_ = ar.alloc(L, BF16)
            x2t_ = ar.alloc(L, BF16)
            B_zt = Buf("zt")
            B_gx1 = Buf("gx1")
            B_gx2 = Buf("gx2")
            decp = [ar.alloc(512, BF16) for _ in range(2)]
            B_decp = [Buf("decp0"), Buf("decp1")]
            NK4 = 9
            NJ = NK4 * 4 * 3
            KI2 = NK4 * 4
            XY = ar.alloc(128 * 64, BF16)
            B_XY = Buf("XY")
            Xv = XY.rearrange("p (f c) -> p f c", c=64)
            Yv5 = XY.rearrange("p (m h l k) -> p m h l k", m=2, h=32, l=2)
            AB = ar.alloc(256 * 32, BF16)
            B_AB = Buf("AB")
            Av3 = AB[:, 0:NJ * 64].rearrange("p (j c) -> p j c", c=64)
            Av4 = AB[:, 0:NJ * 64].rearrange("p (k m c) -> p k m c", m=3, c=64)
            Bv3 = AB.rearrange("p (j q) -> p j q", q=32)
            Bv4 = AB.rearrange("p (m a q) -> p m a q", m=2, a=128)
            Xk = ar.alloc(128 * 64, BF16)
            B_Xk = Buf("Xk")
            Xkv = Xk.rearrange("p (f c) -> p f c", c=64)
            Ak = ar.alloc(NJ * 64, BF16)
            B_Ak = Buf("Ak")
            Akv3 = Ak.rearrange("p (j c) -> p j c", c=64)
            Akv4 = Ak.rearrange("p (k m c) -> p k m c", m=3, c=64)
            Ksp = ar.alloc(NK4 * 512, BF16)
            B_Ksp = Buf("Ksp")
            Kv = Ksp.rearrange("p (k m c) -> p k m c", m=2, c=64)
            tA = [ar.alloc(512, F32) for _ in range(2)]
            tB = [ar.alloc(512, F32) for _ in range(2)]
            B_tA = [Buf("tA0"), Buf("tA1")]
            B_tB = [Buf("tB0"), Buf("tB1")]
            tmpe = [ar.alloc(512, F32) for _ in range(2)]
            B_tmpe = [Buf("tmpe0"), Buf("tmpe1")]
            import os as _os2
            _NG = int(_os2.environ.get('HY_GROUPS', '8'))
            evc = [0]

            def evac(dst, src, R, W):
                actf(dst, src, AF.Copy, R, W)

            def gen_F1(Xsrc, B_X, K, Adst3, B_A, bank0):
                for cp_ in range(16):
                    pb = bank0 + cp_ % 2
                    for i in range(4):
                        c_ = cp_ * 4 + i
                        mm(banks[pb][:, i * NJ:(i + 1) * NJ], Xsrc[0:K, :, c_], D1[0:K, 0:NJ], True, True,
                           (B_X, B_hyc), (PB[pb],))
                    evac(Adst3[:, 0:NJ, cp_ * 4:cp_ * 4 + 4], banks[pb][:, 0:4 * NJ].rearrange("p (c j) -> p j c", c=4),
                         (PB[pb],), (B_A,))
                    yield

            def F2_mm(pb, k4, A4, B_A):
                for i in range(4):
                    k_s = k4 * 4 + i
                    o_ap = banks[pb][:, i * 128:(i + 1) * 128]
                    mm(o_ap, Ere[:, k_s, :], A4[:, k_s, 1:3, :].rearrange("p m c -> p (m c)"), True, False,
                       (B_hyc, B_A), (PB[pb],))
                    mm(o_ap, Eim[:, k_s, :], A4[:, k_s, 0:2, :].rearrange("p m c -> p (m c)"), False, True,
                       (B_hyc, B_A), (PB[pb],))

            def gen_filter12(o, g):
                gc0 = g * 64
                for nf8 in range(16):
                    ds_ = nf8 % 2
                    dma(sp, decp[ds_][0:64, :], hydec_d[g, :, nf8 * 512:(nf8 + 1) * 512], (), (B_decp[ds_],))
                    pb = nf8 % 2
                    for i in range(8):
                        n_f = nf8 * 8 + i
                        wf = (0 * 2 + o) * 512 + gc0
                        wb = (1 * 2 + o) * 512 + gc0
                        mm(banks[pb][0:32, i * 64:(i + 1) * 64], hidT[0:64, n_f:4096:128], w3b[0:64, wf:wf + 64],
                           True, True, (B_hid, B_hyc), (PB[pb],))
                        mm(banks[pb][32:64, i * 64:(i + 1) * 64], hidT[0:64, 4096 + n_f:8192:128], w3b[0:64, wb:wb + 64],
                           True, True, (B_hid, B_hyc), (PB[pb],))
                    tt(dve, Xk[0:64, nf8 * 512:(nf8 + 1) * 512], banks[pb][0:64, :], decp[ds_][0:64, :], ALU.mult,
                       (PB[pb], B_decp[ds_]), (B_Xk,))
                    if nf8 == 0:
                        tt(dve, Xk[0:1, 0:64], Xk[0:1, 0:64], hbrow[0:1, o * 512 + gc0:o * 512 + gc0 + 64], ALU.add,
                           (B_Xk, B_hyc), (B_Xk,))
                    yield
                yield from gen_F1(Xkv, B_Xk, 64, Akv3, B_Ak, 0)

            def gen_filter3(o, g):
                for k4 in range(NK4):
                    pb = k4 % 2
                    F2_mm(pb, k4, Akv4, B_Ak)
                    evac(Ksp[:, k4 * 512:(k4 + 1) * 512], banks[pb][:, :], (PB[pb],), (B_Ksp,))
                    yield

            def gen_data1(o, g):
                gc0 = g * 64
                if o == 0:
                    if g == 0:
                        dma(sp, zt[0:64, :], uT_d[gc0:gc0 + 64, :], (B_uT[gc0 // 128],), (B_zt,))
                elif g + 1 < _NG:
                    gn0 = (g + 1) * 64
                    dma(sp, zt[0:64, :], uT_d[gn0:gn0 + 64, :], (B_uT[gn0 // 128],), (B_zt,))
                for n16 in range(8):
                    pb = 2 + n16 % 2
                    pbv = banks[pb][:, :].bitcast(BF16)
                    for i in range(16):
                        n_f = n16 * 16 + i
                        if o == 0:
                            tr(pbv[0:32, i * 64:(i + 1) * 64], zt[0:64, n_f:4096:128], ident[0:64, 0:64],
                               (B_zt, B_const), (PB[pb],))
                        else:
                            tr(pbv[0:32, i * 64:(i + 1) * 64], zp[0:64, n_f * 32:(n_f + 1) * 32], ident[0:64, 0:64],
                               (B_zp, B_const), (PB[pb],))
                    evac(XY[0:32, n16 * 1024:(n16 + 1) * 1024], pbv[0:32, :], (PB[pb],), (B_XY,))
                    yield
                yield from gen_F1(Xv, B_XY, 32, Av3, B_AB, 2)
                for k4 in range(NK4):
                    pb = 2 + k4 % 2
                    F2_mm(pb, k4, Av4, B_AB)
                    s_ = k4 % 2
                    Uv = banks[pb][:, :].rearrange("p (k m c) -> p k m c", k=4, m=2)
                    Kq = Kv[:, k4 * 4:(k4 + 1) * 4, :, :]
                    tAv = tA[s_].rearrange("p (k m c) -> p k m c", k=4, m=2)
                    tBv = tB[s_].rearrange("p (k m c) -> p k m c", k=4, m=2)
                    tt(dve, tA[s_], banks[pb][:, :], Ksp[:, k4 * 512:(k4 + 1) * 512], ALU.mult,
                       (PB[pb], B_Ksp), (B_tA[s_],))
                    tt(dve, tBv[:, :, 0, :], Uv[:, :, 0, :], Kq[:, :, 1, :], ALU.mult, (PB[pb], B_Ksp), (B_tB[s_],))
                    tt(dve, tBv[:, :, 1, :], Uv[:, :, 1, :], Kq[:, :, 0, :], ALU.mult, (PB[pb], B_Ksp), (B_tB[s_],))
                    yre = Yv5[:, 0, :, :, k4 * 4:(k4 + 1) * 4].rearrange("p h l k -> p l h k")
                    yim = Yv5[:, 1, :, :, k4 * 4:(k4 + 1) * 4].rearrange("p h l k -> p l h k")
                    pm = lambda ap_: ap_.rearrange("p k (l h) -> p l h k", l=2)
                    tt(pool, yre, pm(tAv[:, :, 0, :]), pm(tAv[:, :, 1, :]), ALU.subtract, (B_tA[s_],), (B_XY,))
                    tt(pool, yim, pm(tBv[:, :, 0, :]), pm(tBv[:, :, 1, :]), ALU.add, (B_tB[s_],), (B_XY,))
                    yield

            def gen_data2(o, g):
                gc0 = g * 64
                gate = x1t_ if o == 0 else x2t_
                B_gate = B_gx1 if o == 0 else B_gx2
                if o == 0:
                    dma(sp, x1t_[0:64, :], uT_d[512 + gc0:512 + gc0 + 64, :], (B_uT[(512 + gc0) // 128],), (B_gx1,))
                    dma(sp, x2t_[0:64, :], uT_d[1024 + gc0:1024 + gc0 + 64, :], (B_uT[(1024 + gc0) // 128],), (B_gx2,))
                for q2 in range(16):
                    pb = 2 + q2 % 2
                    for i in range(2):
                        p_ = q2 * 2 + i
                        o_ap = banks[pb][:, i * 256:(i + 1) * 256]
                        mm(o_ap, Yv5[:, 0, p_, :, :].rearrange("p l k -> p (l k)"), Gt[:, 0:256], True, False,
                           (B_XY, B_hyc), (PB[pb],))
                        mm(o_ap, Yv5[:, 1, p_, :, :].rearrange("p l k -> p (l k)"), Gt[:, 256:512], False, True,
                           (B_XY, B_hyc), (PB[pb],))
                    evac(Bv3[:, :, q2 * 2:q2 * 2 + 2], banks[pb][:, :].rearrange("p (q j) -> p j q", q=2), (PB[pb],), (B_AB,))
                    yield
                ztv = zt.rearrange("p (b a) -> p a b", a=128)
                gtv = gate.rearrange("p (b a) -> p a b", a=128)
                for qq in range(4):
                    for cl in range(2):
                        kp = slice(cl * 64, cl * 64 + KI2)
                        op_ = slice(cl * 32, (cl + 1) * 32)
                        for al in range(32):
                            n_a = qq * 32 + al
                            bk = 4 + 2 * cl + (al // 16)
                            o_ap = banks[bk][op_, (al % 16) * 32:(al % 16 + 1) * 32]
                            mm(o_ap, Bv4[kp, 0, n_a, :], Tt[kp, 0, n_a, :], True, False, (B_AB, B_hyc), (PB[bk],))
                            mm(o_ap, Bv4[kp, 1, n_a, :], Tt[kp, 1, n_a, :], False, True, (B_AB, B_hyc), (PB[bk],))
                        yield
                    for cl in range(2):
                        op_ = slice(cl * 32, (cl + 1) * 32)
                        for hb_ in range(2):
                            bk = 4 + 2 * cl + hb_
                            a0 = qq * 32 + hb_ * 16
                            s_ = (cl * 2 + hb_) % 2
                            zpv = zp[op_, a0 * 32:(a0 + 16) * 32].rearrange("p (a b) -> p a b", b=32)
                            tt(dve, zpv, banks[bk][op_, :].rearrange("p (a b) -> p a b", b=32), gtv[op_, a0:a0 + 16, :],
                               ALU.mult, (PB[bk], B_gate), (B_zp,))
                        yield
                if o == 1:
                    zpu = zp.rearrange("p (a b) -> p b a", b=32)
                    ztu = x1t_.rearrange("p (b a) -> p b a", a=128)
                    for hh in range(2):
                        actf(ztu[0:64, hh * 16:(hh + 1) * 16, :], zpu[0:64, hh * 16:(hh + 1) * 16, :], AF.Copy, (B_zp,), (B_gx1,))
                    dma(sp, yhy_d[gc0:gc0 + 64, :], x1t_[0:64, :], (B_gx1,), (B_yhyd[g],))
                    if dbg and g == 0:
                        dma(sp, dbg3_d, x1t_[0:64, :], (B_gx1,), (B_yhyd[g],))
                        dma(sp, dbg4_d, hidT[0:64, :], (B_hid,), (B_yhyd[g],))
                    yield

            def interleave(ga, gb):
                gens = [x for x in (ga, gb) if x is not None]
                while gens:
                    for gen_ in list(gens):
                        try:
                            next(gen_)
                        except StopIteration:
                            gens.remove(gen_)

            items = [(o, g) for g in range(_NG) for o in range(2)]
            for g in range(_NG, 8):
                pass
            interleave(gen_filter12(*items[0]), None)
            interleave(gen_filter3(*items[0]), None)
            for ii, it in enumerate(items):
                nxt = items[ii + 1] if ii + 1 < len(items) else None
                interleave(gen_data1(*it), gen_filter12(*nxt) if nxt else None)
                interleave(gen_data2(*it), gen_filter3(*nxt) if nxt else None)
            for g in range(_NG, 8):
                dma(sp, yhy_d[g * 64:(g + 1) * 64, :], zt[0:64, :], (B_zt,), (B_yhyd[g],))
            ar.pop()
        else:
            ar.push()
            zt = ar.alloc(L, BF16)
            B_zt = Buf("zt")
            memset(pool, zt, 0.0, (B_zt,))
            for g in range(8):
                dma(sp, yhy_d[g * 64:(g + 1) * 64, :], zt[0:64, :], (B_zt,), (B_yhyd[g],))
            ar.pop()

        ar.push()
        Wg = ar.alloc(8 * 2048, BF16).rearrange("p (k n) -> p k n", k=8)
        Wnao = ar.alloc(4 * D, BF16).rearrange("p (k n) -> p k n", k=4)
        Whyo = ar.alloc(4 * D, BF16).rearrange("p (k n) -> p k n", k=4)
        Wout = ar.alloc(8 * D, BF16).rearrange("p (k n) -> p k n", k=8)
        B_wg = [Buf("wg%d" % i) for i in range(4)]
        B_wno = Buf("wno")
        B_who = Buf("who")
        B_wo = [Buf("wo0"), Buf("wo1")]
        winv = win_d.rearrange("(k p) n -> p k n", p=128)
        for hh in (0, 2):
            dma(pool, Wg[:, :, hh * 512:(hh + 1) * 512], winv[:, :, 3072 + hh * 512:3072 + (hh + 1) * 512], (), (B_wg[hh],))
        dma(pool, Wnao, wnao_d.rearrange("(k p) n -> p k n", p=128), (), (B_wno,))
        dma(pool, Whyo, whyo_d.rearrange("(k p) n -> p k n", p=128), (), (B_who,))
        for hh in (1, 3):
            dma(pool, Wg[:, :, hh * 512:(hh + 1) * 512], winv[:, :, 3072 + hh * 512:3072 + (hh + 1) * 512], (), (B_wg[hh],))
        for hh in range(2):
            dma(pool, Wout[:, :, hh * 512:(hh + 1) * 512], wout_d.rearrange("(k p) n -> p k n", p=128)[:, :, hh * 512:(hh + 1) * 512], (), (B_wo[hh],))
        xt4 = [ar.alloc(D, F32) for _ in range(4)]
        B_xt4 = [Buf("xt4_%d" % i) for i in range(4)]
        xn4 = [ar.alloc(D, BF16) for _ in range(4)]
        B_xn4 = [Buf("xn4_%d" % i) for i in range(4)]
        stat4 = [ar.alloc(4, F32) for _ in range(4)]
        B_stat4 = [Buf("st4_%d" % i) for i in range(4)]
        xt4b = [ar.alloc(D, F32) for _ in range(4)]
        B_xt4b = [Buf("xt4b_%d" % i) for i in range(4)]
        xn2 = [ar.alloc(D, BF16) for _ in range(2)]
        B_xn2 = [Buf("xn2_%d" % i) for i in range(2)]
        junk = ar.alloc(D, BF16)
        B_junk = Buf("junk")
        stat = [ar.alloc(4, F32) for _ in range(2)]
        B_stat = [Buf("st0"), Buf("st1")]
        hTc = ar.alloc(8 * 512, BF16).rearrange("p (k t) -> p k t", k=8)
        B_hTc = Buf("hTc")
        hTcb = ar.alloc(8 * 512, BF16).rearrange("p (k t) -> p k t", k=8)
        B_hTcb = Buf("hTcb")
        sg = ar.alloc(4 * 512, BF16).rearrange("p (k t) -> p k t", k=4)
        B_sg = [Buf("sg0"), Buf("sg1")]
        mT = ar.alloc(8 * 512, BF16).rearrange("p (k t) -> p k t", k=8)
        B_mT = Buf("mT")
        tmpa = [ar.alloc(512, F32) for _ in range(2)]
        B_tmpa = [Buf("tmpa0"), Buf("tmpa1")]
        tmpb = [ar.alloc(512, F32) for _ in range(2)]
        B_tmpb = [Buf("tmpb0"), Buf("tmpb1")]
        x1t = [ar.alloc(D, F32) for _ in range(2)]
        B_x1t = [Buf("x1t0"), Buf("x1t1")]
        ynac = [ar.alloc(4 * 512, BF16).rearrange("p (c t) -> p c t", c=4) for _ in range(2)]
        yhyc = [ar.alloc(4 * 512, BF16).rearrange("p (c t) -> p c t", c=4) for _ in range(2)]
        B_ynac = [Buf("ynac0"), Buf("ynac1")]
        B_yhyc = [Buf("yhyc0"), Buf("yhyc1")]
        B_x1d = [Buf("x1d%d" % i) for i in range(NT)]
        x_v = x_d.rearrange("(n p) d -> n p d", p=128)
        x1_v = x1_d.rearrange("(n p) d -> n p d", p=128)
        out_v = out_d.rearrange("(n p) d -> n p d", p=128)
        xt4_all = [xt4, xt4b]
        B_xt4_all = [B_xt4, B_xt4b]
        hTc_all = [hTc, hTcb]
        B_hTc_all = [B_hTc, B_hTcb]

        def p4a_load(ch):
            for i in range(4):
                dma(sp, xt4_all[ch % 2][i], x_v[ch * 4 + i], (), (B_xt4_all[ch % 2][i],))
            dma(sp, ynac[ch % 2], yna_v[:, :, ch * 512:(ch + 1) * 512], (B_ynad,), (B_ynac[ch % 2],))
            dma(sp, yhyc[ch % 2], yhy_v[:, :, ch * 512:(ch + 1) * 512], B_yhyd, (B_yhyc[ch % 2],))

        def p4a_norm(ch):
            for i in range(4):
                norm_tile(xt4_all[ch % 2][i], B_xt4_all[ch % 2][i], xn4[i], B_xn4[i], junk, B_junk, stat4[i], B_stat4[i])

        def p4a_tr(ch, i):
            transpose_mod(xn4[i], B_xn4[i], i % 2, hTc_all[ch % 2], i * 128, B_hTc_all[ch % 2], A1, B1)

        p4a_load(0)
        p4a_norm(0)
        for i in range(4):
            p4a_tr(0, i)
        for ch in range(8):
            xt4 = xt4_all[ch % 2]
            B_xt4 = B_xt4_all[ch % 2]
            hTc = hTc_all[ch % 2]
            B_hTc = B_hTc_all[ch % 2]
            cs_ = ch % 2
            if ch + 1 < 8:
                p4a_load(ch + 1)
                p4a_norm(ch + 1)
            if mode == "dense_test":
                cp(pool, ynac[cs_], hTc[:, 0:4, :], (B_hTc,), (B_ynac[cs_],))
                cp(pool, yhyc[cs_], hTc[:, 4:8, :], (B_hTc,), (B_yhyc[cs_],))
            for dc in range(8):
                s = dc % 2
                for gi, j in enumerate((dc, 8 + dc)):
                    pb = 2 + gi
                    for kc in range(8):
                        mm(banks[pb][:, :], Wg[:, kc, j * 128:(j + 1) * 128], hTc[:, kc, :], kc == 0, kc == 7,
                           (B_wg[(j * 128) // 512], B_hTc), (PB[pb],))
                    actf(sg[:, s * 2 + gi, :], banks[pb][:, :], AF.Sigmoid, (PB[pb],), (B_sg[s],))
                pa, pbb = 4 + 2 * (dc % 2), 5 + 2 * (dc % 2)
                for fc in range(4):
                    mm(banks[pa][:, :], Wnao[:, fc, dc * 128:(dc + 1) * 128], ynac[cs_][:, fc, :],
                       fc == 0, fc == 3, (B_wno, B_ynac[cs_]), (PB[pa],))
                for fc in range(4):
                    mm(banks[pbb][:, :], Whyo[:, fc, dc * 128:(dc + 1) * 128], yhyc[cs_][:, fc, :],
                       fc == 0, fc == 3, (B_who, B_yhyc[cs_]), (PB[pbb],))
                tt(dve, tmpa[s], banks[pa][:, :], sg[:, s * 2, :], ALU.mult, (PB[pa], B_sg[s]), (B_tmpa[s],))
                tt(dve, tmpb[s], banks[pbb][:, :], sg[:, s * 2 + 1, :], ALU.mult, (PB[pbb], B_sg[s]), (B_tmpb[s],))
                tt(pool, mT[:, dc, :], tmpa[s], tmpb[s], ALU.add, (B_tmpa[s], B_tmpb[s]), (B_mT,))
            for i in range(4):
                ti = ch * 4 + i
                s = i % 2
                for hh in range(2):
                    pb = 2 + hh
                    for kc in range(8):
                        mm(banks[pb][:, :], mT[:, kc, i * 128:(i + 1) * 128], Wout[:, kc, hh * 512:(hh + 1) * 512],
                           kc == 0, kc == 7, (B_mT, B_wo[hh]), (PB[pb],))
                    tt(dve, x1t[s][:, hh * 512:(hh + 1) * 512], banks[pb][:, :], g1row[:, hh * 512:(hh + 1) * 512],
                       ALU.mult, (PB[pb], B_mod), (B_x1t[s],))
                tt(pool, x1t[s], x1t[s], xt4[i], ALU.add, (B_x1t[s], B_xt4[i]), (B_x1t[s],))
                dma(sp, x1_v[ti], x1t[s], (B_x1t[s],), (B_x1d[ti],))
                if ch + 1 < 8:
                    p4a_tr(ch + 1, i)
        ar.pop()

        ar.push()
        W1 = ar.alloc(8 * 2816, BF16).rearrange("p (k n) -> p k n", k=8)
        W3 = ar.alloc(8 * 2816, BF16).rearrange("p (k n) -> p k n", k=8)
        W2 = ar.alloc(22 * D, BF16).rearrange("p (k n) -> p k n", k=22)
        B_w5 = Buf("w4b")
        w1v = w1_d.rearrange("(k p) n -> p k n", p=128)
        w3v = w3_d.rearrange("(k p) n -> p k n", p=128)
        w2v = w2_d.rearrange("(k p) n -> p k n", p=128)
        B_w1 = [Buf("w1_%d" % i) for i in range(11)]
        B_w3 = [Buf("w3_%d" % i) for i in range(11)]
        B_w2 = [Buf("w2_%d" % i) for i in range(11)]
        for hh in range(11):
            dma(pool, W1[:, :, hh * 256:(hh + 1) * 256], w1v[:, :, hh * 256:(hh + 1) * 256], (), (B_w1[hh],))
            dma(pool, W3[:, :, hh * 256:(hh + 1) * 256], w3v[:, :, hh * 256:(hh + 1) * 256], (), (B_w3[hh],))
        for hh in range(11):
            dma(pool, W2[:, hh * 2:(hh + 1) * 2, :], w2v[:, hh * 2:(hh + 1) * 2, :], (), (B_w2[hh],))
        xb = [ar.alloc(D, F32) for _ in range(4)]
        B_xb = [Buf("xb%d" % i) for i in range(4)]
        xn2 = [ar.alloc(D, BF16) for _ in range(2)]
        B_xn2 = [Buf("xn2b_%d" % i) for i in range(2)]
        junk = ar.alloc(D, BF16)
        B_junk = Buf("junkb")
        stat = [ar.alloc(4, F32) for _ in range(2)]
        B_stat = [Buf("stb0"), Buf("stb1")]
        h2T = ar.alloc(8 * 256, BF16).rearrange("p (k t) -> p k t", k=8)
        B_h2T = Buf("h2T")
        h2Tb = ar.alloc(8 * 256, BF16).rearrange("p (k t) -> p k t", k=8)
        B_h2Tb = Buf("h2Tb")
        aT = ar.alloc(22 * 256, BF16).rearrange("p (k t) -> p k t", k=22)
        B_aT = Buf("aT")
        sl = [ar.alloc(256, F32) for _ in range(2)]
        B_sl = [Buf("sl0"), Buf("sl1")]
        x2 = [ar.alloc(D, F32) for _ in range(2)]
        B_x2 = [Buf("x2_0"), Buf("x2_1")]
        ot = [ar.alloc(D, F32) for _ in range(2)]
        B_ot = [Buf("ot0"), Buf("ot1")]
        B_outd = Buf("outd")
        fst = [ar.alloc(4, F32) for _ in range(2)]
        B_fst = [Buf("fst0"), Buf("fst1")]
        cnt = 0
        h2T_all = [h2T, h2Tb]
        B_h2T_all = [B_h2T, B_h2Tb]

        def p4b_load(ch):
            for i in range(2):
                ti = ch * 2 + i
                bi = (ch % 2) * 2 + i
                dma(sp, xb[bi], x1_v[ti], (B_x1d[ti],), (B_xb[bi],))

        def p4b_norm(ch):
            for i in range(2):
                bi = (ch % 2) * 2 + i
                norm_tile(xb[bi], B_xb[bi], xn2[i], B_xn2[i], junk, B_junk, stat[i], B_stat[i])

        def p4b_tr(ch, i):
            transpose_mod(xn2[i], B_xn2[i], i, h2T_all[ch % 2], i * 128, B_h2T_all[ch % 2], A2, B2)

        p4b_load(0)
        p4b_norm(0)
        for i in range(2):
            p4b_tr(0, i)
        for ch in range(16):
            h2T = h2T_all[ch % 2]
            B_h2T = B_h2T_all[ch % 2]
            if ch + 1 < 16:
                p4b_load(ch + 1)
                p4b_norm(ch + 1)
            for j in range(22):
                pa, pbb = 2 + 2 * (j % 2), 3 + 2 * (j % 2)
                for kc in range(8):
                    mm(banks[pa][:, 0:256], W1[:, kc, j * 128:(j + 1) * 128], h2T[:, kc, :], kc == 0, kc == 7,
                       (B_w1[j // 2], B_h2T), (PB[pa],))
                for kc in range(8):
                    mm(banks[pbb][:, 0:256], W3[:, kc, j * 128:(j + 1) * 128], h2T[:, kc, :], kc == 0, kc == 7,
                       (B_w3[j // 2], B_h2T), (PB[pbb],))
                s = j % 2
                actf(sl[s], banks[pa][:, 0:256], AF.Silu, (PB[pa],), (B_sl[s],))
                tt(dve, aT[:, j, :], banks[pbb][:, 0:256], sl[s], ALU.mult, (PB[pbb], B_sl[s]), (B_aT,))
            for i in range(2):
                ti = ch * 2 + i
                bi = (ch % 2) * 2 + i
                s = cnt % 2
                cnt += 1
                for hh in range(2):
                    pb = 6 + hh
                    for j in range(22):
                        mm(banks[pb][:, :], aT[:, j, i * 128:(i + 1) * 128], W2[:, j, hh * 512:(hh + 1) * 512],
                           j == 0, j == 21, (B_aT, B_w2[j // 2]), (PB[pb],))
                    tt(dve, x2[s][:, hh * 512:(hh + 1) * 512], banks[pb][:, :], g2row[:, hh * 512:(hh + 1) * 512],
                       ALU.mult, (PB[pb], B_mod), (B_x2[s],))
                tt(pool, x2[s], x2[s], xb[bi], ALU.add, (B_x2[s], B_xb[bi]), (B_x2[s],))
                actf(junk, x2[s], AF.Square, (B_x2[s],), (B_junk, B_fst[s]), accum_out=fst[s][:, 0:1])
                rstd_from_ssq(fst[s], B_fst[s])
                stt(dve, ot[s], x2[s], fst[s][:, 2:3], fgrow, ALU.mult, ALU.mult, (B_x2[s], B_fst[s], B_const), (B_ot[s],))
                dma(sp, out_v[ti], ot[s], (B_ot[s],), (B_outd,))
                if ch + 1 < 16:
                    p4b_tr(ch + 1, i)
        ar.pop()

        if dbg:
            dbt = ar.alloc(4096, F32)
            B_dbt = Buf("dbt")
            memset(pool, dbt, 0.0, (B_dbt,))
            cp(dve, dbt[:, 0:96], modT.rearrange("p c j -> p (c j)"), (B_mod,), (B_dbt,))
            cp(dve, dbt[:, 128:128 + 1024], g1row, (B_mod,), (B_dbt,))
            dma(sp, dbg_d, dbt, (B_dbt,), (B_outd,))

        for t in sp.dtoks[-NDMA:]:
            sp.wait(t)

        with nc.Block() as block:
            @block.tensor
            def _(h):
                pe.replay(h)

            @block.scalar
            def _(h):
                act.replay(h)

            @block.vector
            def _(h):
                dve.replay(h)

            @block.gpsimd
            def _(h):
                pool.replay(h)

            @block.sync
            def _(h):
                sp.replay(h)
    print("instr counts: pe %d act %d dve %d pool %d sp %d(dma %d) pooldma %d" % (
        pe.n, act.n, dve.n, pool.n, sp.n, sp.ndma, pool.ndma))
    return nc


_NC = {}


def make_in_maps(inputs):
    hc = host_consts()
    f = lambda a: np.ascontiguousarray(np.asarray(a, dtype=np.float32))
    x = f(inputs["x"])
    c = f(inputs["c"])
    ctx = f(inputs["ctx"])
    c_ctx = f(inputs["c_ctx"])
    shared = {
        "w_ada": f(inputs["w_ada"][0]),
        "b_adaT": f(np.asarray(inputs["b_ada"][0]).reshape(48, 128).T),
        "n1gT": f(np.asarray(inputs["norm1_g"][0]).reshape(8, 128).T),
        "n2gT": f(np.asarray(inputs["norm2_g"][0]).reshape(8, 128).T),
        "fg_row": f(np.broadcast_to(np.asarray(inputs["final_g"]), (128, D))),
        "w_in": f(inputs["w_in"][0]),
        "w_na_o": f(inputs["w_na_o"][0]),
        "w_hy_o": f(inputs["w_hy_o"][0]),
        "w_out": f(inputs["w_out"][0]),
        "ffn_w1": f(inputs["ffn_w1"][0]),
        "ffn_w3": f(inputs["ffn_w3"][0]),
        "ffn_w2": f(inputs["ffn_w2"][0]),
        "ident": hc["ident"],
        "hy_D1": hc["hy_D1"], "hy_Ere": hc["hy_Ere"], "hy_Eim": hc["hy_Eim"], "hy_G": hc["hy_G"],
        "hy_T": hc["hy_T"], "hy_zT": hc["hy_zT"], "hy_dec": hc["hy_dec"],
        "hy_w1": f(inputs["hy_ffn_w1"][0]), "hy_w2": f(inputs["hy_ffn_w2"][0]), "hy_w3": f(inputs["hy_ffn_w3"][0]),
        "hy_vecs": f(np.stack([np.asarray(inputs["hy_ffn_b1"][0]), np.asarray(inputs["hy_ffn_b2"][0]),
                               np.asarray(inputs["hy_sin_freq"][0])], axis=1)),
        "hy_bias_row": f(np.asarray(inputs["hy_bias"][0]).reshape(1, 1024)),
        "hy_biasT": f(np.asarray(inputs["hy_bias"][0]).reshape(2, 8, 64).transpose(2, 0, 1).reshape(64, 16)),
        "ropecos": hc["ropecos"],
        "ropesin": hc["ropesin"],
        "w_qk_perm": f(np.asarray(inputs["w_in"][0])[:, qk_perm_cols()]),
        "na_biasT": na_bias_table(np.asarray(inputs["na_rpb"][0], dtype=np.float32)),
        "hy_cwT": f(np.asarray(inputs["hy_conv_w"][0]).reshape(3, 12, 128).transpose(2, 1, 0)),
        "hy_cbT": f(np.asarray(inputs["hy_conv_b"][0]).reshape(12, 128).T),
        "identf": hc["identf"],
        "onesf": hc["onesf"],
    }
    maps = []
    for b in range(8):
        m = dict(shared)
        m["x"] = x[b]
        m["ctx"] = ctx[b]
        ccb = np.stack([c[b], c_ctx], axis=1)
        m["cc"] = f(ccb.reshape(8, 128, 2).transpose(1, 0, 2))
        maps.append(m)
    return maps


def kernel(**inputs):
    if "nc" not in _NC:
        _NC["nc"] = build_nc(False)
    nc = _NC["nc"]
    maps = make_in_maps(inputs)
    res = run_bass_kernel_spmd(nc, maps, core_ids=list(range(8)))
    out = np.stack([np.asarray(r["out"], dtype=np.float32) for r in res.results], axis=0)
    return out
```
